# Optimizing a Trainium2 kernel written in Bass

```python
import math
import jax, jax.numpy as jnp
from jax import lax
import numpy as np

D_MODEL = 4096
BATCH = 4
SEQ = 4096
DEPTH = 4

N_MIXERS = 3
N_A = len(range(0, DEPTH, N_MIXERS))
N_B = len(range(1, DEPTH, N_MIXERS))
N_C = len(range(2, DEPTH, N_MIXERS))

A_CHUNK = 128
A_GROUPS = 16
A_WIDTH = D_MODEL
A_GROUP_DIM = A_WIDTH // A_GROUPS

B_PATTERNS = ((128, 1), (512, 4), (2048, 16))
B_HEADS_PER_GROUP = 16
B_HEAD_DIM = 128
B_ROT_DIM = B_HEAD_DIM // 4
ROPE_THETA = 500000.0
B_QBLOCK = 128

C_WINDOWS = (2, 4, 8, 16)
C_WIDTH = D_MODEL
C_GROUP_DIM = C_WIDTH // len(C_WINDOWS)

D_FF = 11008
CONV_WIDTH = 3

LN_EPS = 1e-5
DEEPNORM_ALPHA = (2 * DEPTH) ** 0.25
DEEPNORM_BETA = (8 * DEPTH) ** -0.25

kernel_name = "hybrid_gmlp_dilated_attn_pool_trunk"


def layer_norm(x, g, b):
    xf = x.astype(jnp.float32)
    mu = jnp.mean(xf, axis=-1, keepdims=True)
    var = jnp.mean(jnp.square(xf - mu), axis=-1, keepdims=True)
    y = (xf - mu) * lax.rsqrt(var + LN_EPS)
    return (y * g.astype(jnp.float32) + b.astype(jnp.float32)).astype(x.dtype)


def partial_rotary(t, positions):
    half = B_ROT_DIM // 2
    inv_freq = jnp.float32(ROPE_THETA) ** (-jnp.arange(half, dtype=jnp.float32) * 2.0 / B_ROT_DIM)
    ang = positions.astype(jnp.float32)[..., None] * inv_freq
    cos = jnp.cos(ang)[:, :, None, :]
    sin = jnp.sin(ang)[:, :, None, :]
    tf = t.astype(jnp.float32)
    t1 = tf[..., :half]
    t2 = tf[..., half:B_ROT_DIM]
    out = jnp.concatenate([t1 * cos - t2 * sin, t2 * cos + t1 * sin, tf[..., B_ROT_DIM:]], axis=-1)
    return out.astype(t.dtype)


def dilated_window_attention(q, k, v, window, dilation):
    B, S, H, D = q.shape
    L = S // dilation
    reach = window // dilation
    nb = -(-L // B_QBLOCK)
    Lp = nb * B_QBLOCK
    Z = B * dilation

    def to_blocks(t):
        t = t.reshape(B, L, dilation, H, D).transpose(0, 2, 1, 3, 4).reshape(Z, L, H, D)
        t = jnp.pad(t, ((0, 0), (0, Lp - L), (0, 0), (0, 0)))
        return t.reshape(Z, nb, B_QBLOCK, H, D)

    def with_prev(t):
        prev = jnp.pad(t, ((0, 0), (1, 0), (0, 0), (0, 0), (0, 0)))[:, :nb]
        return jnp.concatenate([prev, t], axis=2)

    def from_blocks(t):
        trailing = t.shape[3:]
        t = t.reshape((Z, Lp) + trailing)[:, :L]
        t = t.reshape((B, dilation, L) + trailing)
        t = jnp.swapaxes(t, 1, 2)
        return t.reshape((B, S) + trailing)

    qb = to_blocks(q)
    kk = with_prev(to_blocks(k))
    vv = with_prev(to_blocks(v))
    s = jnp.einsum('znqhd,znkhd->znhqk', qb, kk, preferred_element_type=jnp.float32) * (D ** -0.5)
    qi = jnp.arange(B_QBLOCK)[:, None]
    kj = jnp.arange(2 * B_QBLOCK)[None, :]
    dist = qi + B_QBLOCK - kj
    band = (dist >= 0) & (dist <= reach)
    key_pos = jnp.arange(nb)[:, None, None] * B_QBLOCK + kj[None] - B_QBLOCK
    mask = band[None] & (key_pos >= 0)
    s = jnp.where(mask[None, :, None], s, -jnp.inf)
    m = jnp.max(s, axis=-1, keepdims=True)
    p = jnp.exp(s - m)
    den = jnp.sum(p, axis=-1, keepdims=True)
    o = jnp.einsum('znhqk,znkhd->znqhd', p, vv.astype(jnp.float32))
    o = o / den.transpose(0, 1, 3, 2, 4)
    lse = (m + jnp.log(den))[..., 0].transpose(0, 1, 3, 2)
    return from_blocks(o), from_blocks(lse)


def dilated_attention_mixer(x, positions, w_in, w_out):
    B, S, _ = x.shape
    G = len(B_PATTERNS)
    qkv = (x @ w_in).reshape(B, S, G, 3, B_HEADS_PER_GROUP, B_HEAD_DIM)
    outs, lses = [], []
    for g, (window, dilation) in enumerate(B_PATTERNS):
        q = partial_rotary(qkv[:, :, g, 0], positions)
        k = partial_rotary(qkv[:, :, g, 1], positions)
        o, lse = dilated_window_attention(q, k, qkv[:, :, g, 2], window, dilation)
        outs.append(o)
        lses.append(lse)
    wts = jax.nn.softmax(jnp.stack(lses), axis=0)
    o = jnp.sum(wts[..., None] * jnp.stack(outs), axis=0)
    return o.reshape(B, S, B_HEADS_PER_GROUP * B_HEAD_DIM).astype(x.dtype) @ w_out


def chunked_gating_mixer(x, w_in, ln_g, ln_b, w_s, b_s, w_out):
    B, S, _ = x.shape
    h = jax.nn.gelu(x @ w_in, approximate=False)
    u, v = jnp.split(h, 2, axis=-1)
    v = layer_norm(v, ln_g, ln_b)
    nc = S // A_CHUNK
    v = v.reshape(B, nc, A_CHUNK, A_GROUPS, A_GROUP_DIM)
    causal = jnp.tril(jnp.ones((A_CHUNK, A_CHUNK), dtype=bool))
    w = jnp.where(causal, w_s, 0.0).astype(v.dtype)
    mixed = jnp.einsum('gts,bnsgc->bntgc', w, v) + b_s.T[None, None, :, :, None].astype(v.dtype)
    y = u * mixed.reshape(B, S, A_WIDTH)
    return y @ w_out


def multiscale_pool_mixer(x, w_in, w_group, scale, w_out):
    B, S, _ = x.shape
    h = (x @ w_in).astype(jnp.float32).reshape(B, S, len(C_WINDOWS), C_GROUP_DIM)
    steps = jnp.arange(1, S + 1, dtype=jnp.float32)
    pooled = []
    for g, window in enumerate(C_WINDOWS):
        hg = h[:, :, g]
        cs = jnp.cumsum(hg, axis=1)
        lagged = jnp.pad(cs, ((0, 0), (window, 0), (0, 0)))[:, :S]
        mean = (cs - lagged) / jnp.minimum(steps, float(window))[None, :, None]
        pooled.append(mean - hg)
    p = jnp.stack(pooled, axis=2).astype(x.dtype)
    y = jnp.einsum('bsgc,gcd->bsgd', p, w_group) * scale.reshape(len(C_WINDOWS), C_GROUP_DIM)
    return y.reshape(B, S, C_WIDTH) @ w_out


def causal_depthwise_conv(h, w, b):
    S = h.shape[1]
    hp = jnp.pad(h, ((0, 0), (CONV_WIDTH - 1, 0), (0, 0)))
    return sum(w[j] * hp[:, j:j + S] for j in range(CONV_WIDTH)) + b


def conv_gated_ffn(x, w_up, conv_w, conv_b, w_down):
    h = causal_depthwise_conv(x @ w_up, conv_w, conv_b)
    gate, up = jnp.split(h, 2, axis=-1)
    return (jax.nn.silu(gate) * up) @ w_down


def setup_inputs(seed: int = 0) -> dict:
    key = jax.random.key(seed)
    ks = iter(jax.random.split(key, 32))

    def nrm(shape, scale):
        return jax.random.normal(next(ks), shape, jnp.float32) * scale

    x = nrm((BATCH, SEQ, D_MODEL), 1.0)
    offset = jax.random.randint(next(ks), (BATCH, 1), 0, 1024, dtype=jnp.int32)
    positions = (jnp.arange(SEQ, dtype=jnp.int32)[None, :] + offset).astype(jnp.int32)
    b_qkv = len(B_PATTERNS) * 3 * B_HEADS_PER_GROUP * B_HEAD_DIM
    b_out = B_HEADS_PER_GROUP * B_HEAD_DIM
    return {
        "x": x,
        "positions": positions,
        "a_w_in": nrm((N_A, D_MODEL, 2 * A_WIDTH), D_MODEL ** -0.5),
        "a_ln_g": 1.0 + nrm((N_A, A_WIDTH), 0.02),
        "a_ln_b": nrm((N_A, A_WIDTH), 0.02),
        "a_w_s": nrm((N_A, A_GROUPS, A_CHUNK, A_CHUNK), A_CHUNK ** -0.5),
        "a_b_s": 1.0 + nrm((N_A, A_GROUPS, A_CHUNK), 0.02),
        "a_w_out": nrm((N_A, A_WIDTH, D_MODEL), A_WIDTH ** -0.5 * DEEPNORM_BETA),
        "b_w_in": nrm((N_B, D_MODEL, b_qkv), D_MODEL ** -0.5),
        "b_w_out": nrm((N_B, b_out, D_MODEL), b_out ** -0.5 * DEEPNORM_BETA),
        "c_w_in": nrm((N_C, D_MODEL, C_WIDTH), D_MODEL ** -0.5),
        "c_w_group": nrm((N_C, len(C_WINDOWS), C_GROUP_DIM, C_GROUP_DIM), C_GROUP_DIM ** -0.5),
        "c_scale": 1.0 + nrm((N_C, C_WIDTH), 0.02),
        "c_w_out": nrm((N_C, C_WIDTH, D_MODEL), C_WIDTH ** -0.5 * DEEPNORM_BETA),
        "f_w_up": nrm((DEPTH, D_MODEL, 2 * D_FF), D_MODEL ** -0.5),
        "f_conv_w": nrm((DEPTH, CONV_WIDTH, 2 * D_FF), CONV_WIDTH ** -0.5),
        "f_conv_b": nrm((DEPTH, 2 * D_FF), 0.02),
        "f_w_down": nrm((DEPTH, D_FF, D_MODEL), D_FF ** -0.5 * DEEPNORM_BETA),
        "ln_mix_g": 1.0 + nrm((DEPTH, D_MODEL), 0.02),
        "ln_mix_b": nrm((DEPTH, D_MODEL), 0.02),
        "ln_ffn_g": 1.0 + nrm((DEPTH, D_MODEL), 0.02),
        "ln_ffn_b": nrm((DEPTH, D_MODEL), 0.02),
    }


def reference(x, positions, a_w_in, a_ln_g, a_ln_b, a_w_s, a_b_s, a_w_out,
              b_w_in, b_w_out, c_w_in, c_w_group, c_scale, c_w_out,
              f_w_up, f_conv_w, f_conv_b, f_w_down,
              ln_mix_g, ln_mix_b, ln_ffn_g, ln_ffn_b):
    for i in range(DEPTH):
        kind, j = i % N_MIXERS, i // N_MIXERS
        if kind == 0:
            y = chunked_gating_mixer(x, a_w_in[j], a_ln_g[j], a_ln_b[j], a_w_s[j], a_b_s[j], a_w_out[j])
        elif kind == 1:
            y = dilated_attention_mixer(x, positions, b_w_in[j], b_w_out[j])
        else:
            y = multiscale_pool_mixer(x, c_w_in[j], c_w_group[j], c_scale[j], c_w_out[j])
        x = layer_norm(DEEPNORM_ALPHA * x + y, ln_mix_g[i], ln_mix_b[i])
        f = conv_gated_ffn(x, f_w_up[i], f_conv_w[i], f_conv_b[i], f_w_down[i])
        x = layer_norm(DEEPNORM_ALPHA * x + f, ln_ffn_g[i], ln_ffn_b[i])
    return x
```

```python
import math
import numpy as np
import ml_dtypes
from contextlib import ExitStack
import concourse.bass as bass
import concourse.mybir as mybir
from concourse.bass_utils import run_bass_kernel_spmd

F32 = mybir.dt.float32
BF16 = mybir.dt.bfloat16
I32 = mybir.dt.int32
AF = mybir.ActivationFunctionType
ALU = mybir.AluOpType

CFG = dict(D=4096, S=4096, DFF=11008, H=16, AG=16, DEPTH=4, NB=4)
TP = 1024
TT = 512
LN_EPS = 1e-5
ROPE_THETA = 500000.0
PATTERNS = ((128, 1), (512, 4), (2048, 16))
C_WINDOWS = (2, 4, 8, 16)


class Sem:
    _n = 0

    def __init__(self, h):
        self.h = h; self.cnt = 0; Sem._n += 1; self.uid = Sem._n


class Buf:
    __slots__ = ("t", "w", "r", "dsem", "name")

    def __init__(self, t, name, dsem=None):
        self.t = t; self.w = {}; self.r = {}; self.dsem = dsem; self.name = name

    def __getitem__(self, k):
        return self.t[k]


class Prog:
    def __init__(self, nc, es):
        self.nc = nc; self.es = es
        self.eng = {"pe": nc.tensor, "act": nc.scalar, "dve": nc.vector, "pool": nc.gpsimd, "sp": nc.sync}
        self.main_es = es
        self.psem = {}
        for e in ("pe", "act", "dve", "pool"):
            self.psem[e] = Sem(es.enter_context(nc.semaphore("prog_" + e)))
        self.waited = {e: {} for e in self.eng}
        self.nsem = 4
        self.allbufs = []
        self.sempool = []
        self.allsems = []

    def get_dsem(self):
        if self.sempool:
            return self.sempool.pop()
        self.nsem += 1
        sm = Sem(self.main_es.enter_context(self.nc.semaphore("dsem%d" % self.nsem)))
        self.allsems.append(sm)
        return sm

    def scope(self):
        P = self

        class _S:
            def __enter__(s2):
                s2.old = P.es; s2.es = ExitStack(); P.es = s2.es; s2.n = len(P.allbufs); return s2

            def __exit__(s2, *a):
                P.barrier()
                for b in P.allbufs[s2.n:]:
                    if b.dsem is not None:
                        P.sempool.append(b.dsem)
                del P.allbufs[s2.n:]
                s2.es.close(); P.es = s2.old
                return False
        return _S()

    def sb(self, name, shape, dt, dma=True):
        t = self.es.enter_context(self.nc.sbuf_tensor(name, list(shape), dt))
        ds = self.get_dsem() if dma else None
        b = Buf(t, name, ds); self.allbufs.append(b); return b

    def view(self, ap, name, dma=True):
        b = Buf(ap, name, self.get_dsem() if dma else None); self.allbufs.append(b); return b

    def ps(self, name, shape, dt):
        t = self.es.enter_context(self.nc.psum_tensor(name, list(shape), dt))
        b = Buf(t, name); self.allbufs.append(b); return b

    def dram(self, name, shape, dt):
        t = self.nc.dram_tensor(name, list(shape), dt)
        b = Buf(t.ap(), name); self.allbufs.append(b); return b

    def dram_in(self, name, shape, dt):
        t = self.nc.dram_tensor(name, list(shape), dt, kind="ExternalInput")
        return Buf(t.ap(), name)

    def wait(self, e, tok):
        sem, val = tok
        k = sem.uid
        if self.waited[e].get(k, 0) >= val:
            return
        self.eng[e].wait_ge(sem.h, val)
        self.waited[e][k] = val

    def deps(self, e, reads, writes):
        for b in reads:
            for t in b.w.values():
                self.wait(e, t)
        for b in writes:
            for t in b.w.values():
                self.wait(e, t)
            for t in b.r.values():
                self.wait(e, t)

    def commit(self, tok, reads, writes):
        k = tok[0].uid
        for b in reads:
            b.r[k] = tok
        for b in writes:
            b.w = {k: tok}; b.r = {}

    def op(self, e, fn, reads=(), writes=()):
        self.deps(e, reads, writes)
        ins = fn(self.eng[e])
        sm = self.psem[e]; sm.cnt += 1
        ins.then_inc(sm.h, 1)
        tok = (sm, sm.cnt)
        self.commit(tok, reads, writes)
        return tok

    def mm(self, fns, reads, writes):
        self.deps("pe", reads, writes)
        ins = None
        for f in fns:
            ins = f(self.nc.tensor)
        sm = self.psem["pe"]; sm.cnt += 1
        ins.then_inc(sm.h, 1)
        tok = (sm, sm.cnt)
        self.commit(tok, reads, writes)
        return tok

    def dma(self, q, out_ap, in_ap, sbuf, reads=(), writes=(), **kw):
        self.deps(q, reads, writes)
        ins = self.eng[q].dma_start(out=out_ap, in_=in_ap, **kw)
        sm = sbuf.dsem; sm.cnt += 16
        ins.then_inc(sm.h, 16)
        tok = (sm, sm.cnt)
        self.commit(tok, reads, writes)
        return tok

    def barrier(self):
        toks = []
        for e in ("pe", "act", "dve", "pool"):
            if self.psem[e].cnt:
                toks.append((self.psem[e], self.psem[e].cnt))
        for sm in self.allsems:
            if sm.cnt:
                toks.append((sm, sm.cnt))
        for e in self.eng:
            for t in toks:
                self.wait(e, t)


def cfg_derive(c):
    c = dict(c)
    c["DC"] = c["D"] // 128
    c["FC"] = c["DFF"] // 128
    c["NQKV"] = 3 * 3 * c["H"]
    c["AGC"] = (c["D"] // c["AG"]) // 128
    c["CGC"] = (c["D"] // 4) // 128
    c["NPASS"] = c["S"] // TP
    c["NA"] = len(range(0, c["DEPTH"], 3)); c["NBm"] = len(range(1, c["DEPTH"], 3)); c["NCm"] = len(range(2, c["DEPTH"], 3))
    return c


def build(cfg, layers=None):
    c = cfg_derive(cfg)
    D, S, DFF, H, DC, FC = c["D"], c["S"], c["DFF"], c["H"], c["DC"], c["FC"]
    AG, AGC, CGC, NPASS, DEPTH = c["AG"], c["AGC"], c["CGC"], c["NPASS"], c["DEPTH"]
    NA, NBm, NCm = c["NA"], c["NBm"], c["NCm"]
    layers = list(range(DEPTH)) if layers is None else list(layers)
    STAGE = cfg.get("STAGE")
    HAL = cfg.get("HAL", 0)
    NT = cfg.get("NT", S)
    SL = HAL + NT
    NK = cfg.get("NK", NT)
    PASSES_OWN = [(HAL + i * TP, TP // TT) for i in range(NT // TP)]
    assert HAL == 0 or HAL <= TT
    PASSES_HALO = ([(0, 1)] if HAL else []) + PASSES_OWN
    cur = {"T0p": 0}
    NL = len(layers)
    ja = [l // 3 for l in layers if l % 3 == 0]; jb = [l // 3 for l in layers if l % 3 == 1]; jc = [l // 3 for l in layers if l % 3 == 2]
    NPARTS = 3 if FC > 2 * DC else 2
    KCMAX = max(DC, -(-FC // NPARTS), H)
    alpha = float((2 * DEPTH) ** 0.25)
    nc = bass.Bass("TRN2", target_bir_lowering=False)
    es = ExitStack()
    P = Prog(nc, es)

    xT_in = P.dram_in("xT", [D, SL], F32)
    pos_in = P.dram_in("pos", [1, NT], I32) if (jb and STAGE in (None, "qkv")) else None
    W = {}
    need_mix = STAGE in (None, "mix", "qkv", "att")
    need_ffn = STAGE in (None, "ffn")
    if ja and need_mix:
        W["a_w_in"] = P.dram_in("a_w_in", [len(ja), D, 2 * D], F32)
        W["a_w_out"] = P.dram_in("a_w_out", [len(ja), D, D], F32)
        W["a_lnv"] = P.dram_in("a_lnv", [len(ja), 128, 2 * DC], F32)
        W["a_wsT"] = P.dram_in("a_wsT", [len(ja), 128, AG * 128], F32)
        W["a_bsb"] = P.dram_in("a_bsb", [len(ja), 128, AG * 128], F32)
    if jb and STAGE in (None, "qkv"):
        W["b_w_in"] = P.dram_in("b_w_in", [len(jb), D, 9 * H * 128], F32)
    if jb and STAGE in (None, "att"):
        W["b_w_out"] = P.dram_in("b_w_out", [len(jb), H * 128, D], F32)
    if jc and need_mix:
        W["c_w_in"] = P.dram_in("c_w_in", [len(jc), D, D], F32)
        W["c_w_group"] = P.dram_in("c_w_group", [len(jc), 4, D // 4, D // 4], F32)
        W["c_w_out"] = P.dram_in("c_w_out", [len(jc), D, D], F32)
        W["c_scale"] = P.dram_in("c_scale", [len(jc), 128, DC], F32)
    if need_ffn:
        W["f_w_up"] = P.dram_in("f_w_up", [NL, D, 2 * DFF], F32)
        W["f_w_down"] = P.dram_in("f_w_down", [NL, DFF, D], F32)
        W["f_cv"] = P.dram_in("f_cv", [NL, 128, 8 * FC], F32)
    if STAGE != "qkv":
        W["lnv"] = P.dram_in("lnv", [NL, 128, 4 * DC], F32)
    cst_f = P.dram_in("cst_f", [128, 128 + 128 + 1 + 4 * TT], F32)
    cst_b = P.dram_in("cst_b", [128, 128 + 256 + 256], BF16)
    if STAGE != "qkv":
        out_t = nc.dram_tensor("outT", [D, NT], F32, kind="ExternalOutput")
        out_b = Buf(out_t.ap(), "outT")
    else:
        out_b = None

    xr = P.dram("xr", [D, SL], F32)
    actA = P.dram("actA", [max(D, DFF), SL], BF16)
    actB = P.dram("actB", [D, SL], BF16)
    vT = P.dram("vT", [D, SL], F32)
    QR = 3 * H * 128
    if STAGE == "qkv":
        q_dst = Buf(nc.dram_tensor("q_out", [QR, NT], BF16, kind="ExternalOutput").ap(), "q_out")
        k_dst = Buf(nc.dram_tensor("k_out", [QR, NT], BF16, kind="ExternalOutput").ap(), "k_out")
        v_dst = Buf(nc.dram_tensor("v_out", [QR, NT], BF16, kind="ExternalOutput").ap(), "v_out")
        q_src = k_src = v_src = None
    elif STAGE == "att":
        q_src = P.dram_in("q_in", [QR, NT], BF16)
        k_src = P.dram_in("k_in", [QR, NK], BF16)
        v_src = P.dram_in("v_in", [QR, NK], BF16)
        q_dst = k_dst = v_dst = None
    else:
        assert NK == NT
        q_dst = q_src = P.dram("qT", [QR, NT], BF16)
        k_dst = k_src = P.dram("kT", [QR, NT], BF16)
        v_dst = v_src = P.dram("vvT", [QR, NT], BF16)
    csd = P.dram("csd", [2 * 128, NT], F32)

    ident = P.sb("ident", [128, 128], BF16)
    maskb = P.sb("maskb", [128, 256], BF16)
    maskp = P.sb("maskp", [128, 256], BF16)
    Rm = P.sb("Rm", [128, 128], F32)
    tril = P.sb("tril", [128, 128], F32)
    invf = P.sb("invf", [128, 1], F32)
    ones = P.sb("ones", [128, 128], F32, dma=False)
    onesb = P.sb("onesb", [128, 128], BF16, dma=False)
    s1 = P.sb("s1", [128, TP], F32, dma=False)
    s2 = P.sb("s2", [128, TP], F32, dma=False)
    lnv = P.sb("lnv_s", [128, 4 * DC], F32)
    ARENA = max(KCMAX * TP // 2, 2 * NT + NT // 2 + 3 * (NK // 2))
    arena = P.sb("arena", [128, ARENA], F32, dma=False)
    xs = P.view(arena.t[:, 0:KCMAX * TP // 2].bitcast(BF16).rearrange("p (c t) -> p c t", t=TP), "xs")
    WBW = cfg.get("WBW", 512 if (STAGE in ("ffn", "qkv", "att") or (STAGE == "mix" and not ja)) else 256)
    wb = [P.sb("wb%d" % i, [128, KCMAX, WBW], BF16) for i in range(2)]
    NSCR = 12
    scr = [P.sb("scr%d" % i, [128, 528], F32) for i in range(NSCR)]
    NSB = 4
    scb = [P.sb("scb%d" % i, [128, 512], BF16) for i in range(NSB)]
    pg = [P.ps("pg%d" % i, [128, TT], F32) for i in range(4)]
    pm = [P.ps("pm%d" % i, [128, TT], F32) for i in range(3)]
    pt = P.ps("ptb", [128, TT], BF16)

    P.dma("sp", ident[:, :], cst_b[:, 0:128], ident, writes=[ident])
    P.dma("sp", maskb[:, :], cst_b[:, 128:384], maskb, writes=[maskb])
    P.dma("sp", maskp[:, :], cst_b[:, 384:640], maskp, writes=[maskp])
    P.dma("sp", Rm[:, :], cst_f[:, 0:128], Rm, writes=[Rm])
    P.dma("sp", tril[:, :], cst_f[:, 128:256], tril, writes=[tril])
    P.dma("sp", invf[:, :], cst_f[:, 256:257], invf, writes=[invf], allow_slow_non_contiguous=True)
    P.op("dve", lambda e: e.memset(ones[:, :], 1.0), writes=[ones])
    P.op("dve", lambda e: e.memset(onesb[:, :], 1.0), writes=[onesb])

    rr = {"f": 0, "b": 0, "g": 0, "m": 0}

    def getf():
        rr["f"] = (rr["f"] + 1) % NSCR; return scr[rr["f"]]

    def getb():
        rr["b"] = (rr["b"] + 1) % NSB; return scb[rr["b"]]

    def getpm():
        rr["m"] = (rr["m"] + 1) % 3; return pm[rr["m"]]

    xin = xr if STAGE is None else xT_in

    def gemm(src, src_f32, K0, KC, groups, ep, pre=None, on_pass_start=None, on_pass_end=None, passes=None):
        srcv = src.t[K0:K0 + KC * 128, :].rearrange("(c p) t -> p c t", p=128)
        for p in (passes if passes is not None else PASSES_OWN):
            T0p, ntile = p
            cur["T0p"] = T0p
            tw = HAL if (HAL and T0p == 0) else TT
            q = "pool" if src_f32 else "sp"
            step = 8
            for c0 in range(0, KC, step):
                c1 = min(KC, c0 + step)
                P.dma(q, xs[:, c0:c1, 0:ntile * tw], srcv[:, c0:c1, T0p:T0p + ntile * tw], xs, reads=[src], writes=[xs])
            if on_pass_start:
                on_pass_start(p)

            def loadw(sgi):
                slot = wb[sgi % 2]
                for wap, off in groups[sgi]["loads"]:
                    wv = wap.rearrange("(c p) n -> p c n", p=128)
                    P.dma("pool", slot[:, 0:KC, off:off + wap.shape[1]], wv, slot, writes=[slot])
            loadw(0)
            units = []
            for sgi in range(len(groups)):
                for u in groups[sgi]["units"]:
                    units.append((sgi, u))
            for gi, (sgi, unit) in enumerate(units):
                if (gi == 0 or units[gi - 1][0] != sgi) and sgi + 1 < len(groups):
                    loadw(sgi + 1)
                slot = wb[sgi % 2]
                CH = len(unit)
                for tt in range(ntile):
                    T0 = T0p + tt * tw
                    if pre:
                        pre(gi, T0)
                    psums = []
                    for j in range(CH):
                        woff = unit[j]
                        rr["g"] = (rr["g"] + 1) % 4
                        pb = pg[rr["g"]]
                        psums.append(pb)
                        fns = []
                        if True:
                            for k in range(KC):
                                fns.append(lambda e, k=k, woff=woff, pb=pb, tt=tt, slot=slot, tw=tw: e.matmul(
                                    pb[:, 0:tw], slot[:, k, woff:woff + 128], xs[:, k, tt * tw:(tt + 1) * tw],
                                    start=(k == 0), stop=(k == KC - 1)))
                        P.mm(fns, reads=[slot, xs], writes=[pb])
                    ep(gi, T0, psums)
            if on_pass_end:
                on_pass_end(p)

    def wcols(wap2d, n0, K0=0, KC=None):
        KC = KC if KC is not None else wap2d.shape[0] // 128
        return wap2d[K0:K0 + KC * 128, n0:n0 + 128]

    def pair_groups(wap2d, nchunks, K0=0, KC=None):
        KCx = KC if KC is not None else wap2d.shape[0] // 128
        g = []
        per = WBW // 128
        for n in range(0, nchunks, per):
            w = min(per, nchunks - n)
            loads = [(wap2d[K0:K0 + KCx * 128, n * 128:(n + w) * 128], 0)]
            units = [[(u + j) * 128 for j in range(min(2, w - u))] for u in range(0, w, 2)]
            g.append(dict(loads=loads, units=units))
        return g

    def stats_reset(p):
        P.op("dve", lambda e: e.memset(s1[:, :], 0.0), writes=[s1])
        P.op("dve", lambda e: e.memset(s2[:, :], 0.0), writes=[s2])

    def stats_add(zb, zap, T0):
        c0 = T0 - cur["T0p"]
        sq = getf()
        P.op("act", lambda e: e.activation(out=sq[:, 0:TT], in_=zap, func=AF.Square), reads=[zb], writes=[sq])
        P.op("dve", lambda e: e.tensor_tensor(out=s1[:, c0:c0 + TT], in0=s1[:, c0:c0 + TT], in1=zap, op=ALU.add),
             reads=[zb, s1], writes=[s1])
        P.op("dve", lambda e: e.tensor_tensor(out=s2[:, c0:c0 + TT], in0=s2[:, c0:c0 + TT], in1=sq[:, 0:TT], op=ALU.add),
             reads=[sq, s2], writes=[s2])

    def stats_finalize(nfeat):
        inv = 1.0 / nfeat
        for tt in range(TP // TT):
            sl = slice(tt * TT, (tt + 1) * TT)
            pa = getpm(); pbv = getpm()
            P.mm([lambda e, pa=pa, sl=sl: e.matmul(pa[:, :], ones[:, :], s1[:, sl], start=True, stop=True)],
                 reads=[ones, s1], writes=[pa])
            P.mm([lambda e, pbv=pbv, sl=sl: e.matmul(pbv[:, :], ones[:, :], s2[:, sl], start=True, stop=True)],
                 reads=[ones, s2], writes=[pbv])
            mean = getf(); tmp = getf()
            P.op("dve", lambda e: e.tensor_scalar(out=mean[:, 0:TT], in0=pa[:, :], scalar1=inv, scalar2=None, op0=ALU.mult),
                 reads=[pa], writes=[mean])
            P.op("dve", lambda e: e.tensor_tensor(out=tmp[:, 0:TT], in0=mean[:, 0:TT], in1=mean[:, 0:TT], op=ALU.mult),
                 reads=[mean], writes=[tmp])
            P.op("dve", lambda e: e.scalar_tensor_tensor(out=tmp[:, 0:TT], in0=pbv[:, :], scalar=inv, in1=tmp[:, 0:TT],
                                                         op0=ALU.mult, op1=ALU.subtract), reads=[pbv, tmp], writes=[tmp])
            P.op("dve", lambda e: e.tensor_scalar(out=tmp[:, 0:TT], in0=tmp[:, 0:TT], scalar1=1.0, scalar2=LN_EPS,
                                                  op0=ALU.mult, op1=ALU.add), reads=[tmp], writes=[tmp])
            P.op("act", lambda e: e.activation(out=tmp[:, 0:TT], in_=tmp[:, 0:TT], func=AF.Sqrt), reads=[tmp], writes=[tmp])
            P.op("dve", lambda e: e.reciprocal(out=s2[:, sl], in_=tmp[:, 0:TT]), reads=[tmp, s2], writes=[s2])
            P.op("dve", lambda e: e.scalar_tensor_tensor(out=s1[:, sl], in0=mean[:, 0:TT], scalar=-1.0, in1=s2[:, sl],
                                                         op0=ALU.mult, op1=ALU.mult), reads=[mean, s2, s1], writes=[s1])

    def normalize_tile(zb, zap, c0, gap, bap, outb, outap, gbbuf=None):
        t = getf()
        P.op("dve", lambda e: e.tensor_tensor(out=t[:, 0:TT], in0=zap, in1=s2[:, c0:c0 + TT], op=ALU.mult),
             reads=[zb, s2], writes=[t])
        P.op("dve", lambda e: e.tensor_tensor(out=t[:, 0:TT], in0=t[:, 0:TT], in1=s1[:, c0:c0 + TT], op=ALU.add),
             reads=[t, s1], writes=[t])
        P.op("act", lambda e: e.activation(out=outap, in_=t[:, 0:TT], func=AF.Identity, scale=gap, bias=bap),
             reads=[t, gbbuf if gbbuf is not None else lnv], writes=[outb])

    def ln_sweep(p, goff, boff, dst):
        stats_finalize(D)
        for cch in range(DC):
            for tt in range(p[1]):
                T0 = p[0] + tt * TT
                zt = getf()
                P.dma("sp", zt[:, 0:TT], xr.t[cch * 128:(cch + 1) * 128, T0:T0 + TT], zt, reads=[xr], writes=[zt])
                xn = getf()
                normalize_tile(zt, zt[:, 0:TT], tt * TT, lnv[:, goff + cch:goff + cch + 1],
                               lnv[:, boff + cch:boff + cch + 1], xn, xn[:, 0:TT])
                Td = T0 - HAL if dst is out_b else T0
                P.dma("sp", dst.t[cch * 128:(cch + 1) * 128, Td:Td + TT], xn[:, 0:TT], xn, reads=[xn], writes=[dst])

    def make_residual(scal, with_stats, rsrc=None):
        st = {}
        rsrc = xr if rsrc is None else rsrc

        def pre(gi, T0):
            tiles = []
            for j in range(2):
                cch = gi * 2 + j
                if cch >= DC:
                    break
                xin_t = getf()
                P.dma("sp", xin_t[:, 0:TT], rsrc.t[cch * 128:(cch + 1) * 128, T0:T0 + TT], xin_t, reads=[rsrc], writes=[xin_t])
                tiles.append(xin_t)
            st[(gi, T0)] = tiles

        def ep(gi, T0, psums):
            tiles = st.pop((gi, T0))
            for j, pb in enumerate(psums):
                cch = gi * 2 + j
                xin_t = tiles[j]
                z = getf()
                P.op("dve", lambda e: e.scalar_tensor_tensor(out=z[:, 0:TT], in0=xin_t[:, 0:TT], scalar=scal, in1=pb[:, :],
                                                             op0=ALU.mult, op1=ALU.add), reads=[xin_t, pb], writes=[z])
                if with_stats:
                    stats_add(z, z[:, 0:TT], T0)
                P.dma("sp", xr.t[cch * 128:(cch + 1) * 128, T0:T0 + TT], z[:, 0:TT], z, reads=[z], writes=[xr])
        return pre, ep

    def ffn(layer, is_last):
        li = layers.index(layer)
        if not hasattr(ffn, "cv"):
            ffn.cv = P.sb("cv", [128, 8 * FC], F32)
            ffn.halo = P.sb("fhalo", [128, 2 * FC, 2], F32, dma=False)
        cv = ffn.cv; halo = ffn.halo
        P.dma("sp", cv[:, :], W["f_cv"].t[li], cv, writes=[cv])
        P.dma("sp", lnv[:, :], W["lnv"].t[li], lnv, writes=[lnv])
        wup = W["f_w_up"].t[li]
        groups = []
        per = WBW // 256
        for i in range(0, FC, per):
            w = min(per, FC - i)
            groups.append(dict(loads=[(wup[:, i * 128:(i + w) * 128], 0), (wup[:, DFF + i * 128:DFF + (i + w) * 128], w * 128)],
                               units=[[u * 128, (w + u) * 128] for u in range(w)]))
        P.op("dve", lambda e: e.memset(halo[:, :, :], 0.0), writes=[halo])

        def conv(pb, ch, T0):
            hb = getf()
            P.op("act", lambda e: e.activation(out=hb[:, 2:2 + TT], in_=pb[:, :], func=AF.Copy), reads=[pb], writes=[hb])
            P.op("dve", lambda e: e.tensor_copy(out=hb[:, 0:2], in_=halo[:, ch, :]), reads=[halo, hb], writes=[hb])
            P.op("dve", lambda e: e.tensor_copy(out=halo[:, ch, :], in_=hb[:, TT:TT + 2]), reads=[hb, halo], writes=[halo])
            acc = getf()
            w = lambda j: cv[:, j * 2 * FC + ch:j * 2 * FC + ch + 1]
            P.op("act", lambda e: e.activation(out=acc[:, 0:TT], in_=hb[:, 2:2 + TT], func=AF.Identity, scale=w(2), bias=w(3)),
                 reads=[hb, cv], writes=[acc])
            P.op("dve", lambda e: e.scalar_tensor_tensor(out=acc[:, 0:TT], in0=hb[:, 1:1 + TT], scalar=w(1), in1=acc[:, 0:TT],
                                                         op0=ALU.mult, op1=ALU.add), reads=[hb, cv, acc], writes=[acc])
            P.op("dve", lambda e: e.scalar_tensor_tensor(out=acc[:, 0:TT], in0=hb[:, 0:TT], scalar=w(0), in1=acc[:, 0:TT],
                                                         op0=ALU.mult, op1=ALU.add), reads=[hb, cv, acc], writes=[acc])
            return acc

        def ep_up(gi, T0, psums):
            if T0 < HAL:
                for pb_, ch in ((psums[0], gi), (psums[1], FC + gi)):
                    P.op("dve", lambda e, pb_=pb_, ch=ch: e.tensor_copy(out=halo[:, ch, :], in_=pb_[:, HAL - 2:HAL]),
                         reads=[pb_, halo], writes=[halo])
                return
            cg = conv(psums[0], gi, T0)
            cu = conv(psums[1], FC + gi, T0)
            sg = getf()
            P.op("act", lambda e: e.activation(out=sg[:, 0:TT], in_=cg[:, 0:TT], func=AF.Silu), reads=[cg], writes=[sg])
            ab = getb()
            P.op("dve", lambda e: e.tensor_tensor(out=ab[:, :], in0=sg[:, 0:TT], in1=cu[:, 0:TT], op=ALU.mult),
                 reads=[sg, cu], writes=[ab])
            P.dma("sp", actA.t[gi * 128:(gi + 1) * 128, T0:T0 + TT], ab[:, :], ab, reads=[ab], writes=[actA])

        gemm(xin, True, 0, DC, groups, ep_up, passes=PASSES_HALO)
        wdn = W["f_w_down"].t[li]
        dst = out_b if is_last else xr
        per = -(-FC // NPARTS)
        k0 = 0
        for part in range(NPARTS):
            kc = min(per, FC - k0)
            last = part == NPARTS - 1
            prex, epx = make_residual(alpha if part == 0 else 1.0, last, rsrc=(xin if part == 0 else xr))
            gemm(actA, False, k0 * 128, kc, pair_groups(wdn, DC, k0 * 128, kc), epx, pre=prex,
                 on_pass_start=stats_reset if last else None,
                 on_pass_end=(lambda p: ln_sweep(p, 2 * DC, 3 * DC, dst)) if last else None)
            k0 += kc

    def out_proj(src, KC, wap, layer):
        P.dma("sp", lnv[:, :], W["lnv"].t[layers.index(layer)], lnv, writes=[lnv])
        pre, ep = make_residual(alpha, True, rsrc=xin)
        gemm(src, False, 0, KC, pair_groups(wap, DC, 0, KC), ep, pre=pre,
             on_pass_start=stats_reset,
             on_pass_end=lambda p: ln_sweep(p, 0, DC, out_b if STAGE in ("mix", "att") else xr))

    def mixer_a(layer, j):
        with P.scope():
            mixer_a_body(layer, j)
        out_proj(actB, DC, W["a_w_out"].t[j], layer)

    def mixer_a_body(layer, j):
        alnv = P.sb("alnv%d" % j, [128, 2 * DC], F32)
        wsb = P.sb("wsb%d" % j, [128, AG * 128], BF16, dma=False)
        bsb = P.sb("bsb%d" % j, [128, AG * 128], F32)
        vtok = P.sb("vtok%d" % j, [128, 4, D], BF16, dma=False)
        P.dma("sp", alnv[:, :], W["a_lnv"].t[j], alnv, writes=[alnv])
        P.dma("sp", bsb[:, :], W["a_bsb"].t[j], bsb, writes=[bsb])
        for g in range(AG):
            wsf = getf()
            P.dma("sp", wsf[:, 0:128], W["a_wsT"].t[j, :, g * 128:(g + 1) * 128], wsf, writes=[wsf])
            P.op("dve", lambda e, g=g, wsf=wsf: e.tensor_tensor(out=wsb[:, g * 128:(g + 1) * 128], in0=wsf[:, 0:128],
                                                                 in1=tril[:, :], op=ALU.mult), reads=[wsf, tril], writes=[wsb])
        win = W["a_w_in"].t[j]

        def ep_in(gi, T0, psums):
            for jj, pb in enumerate(psums):
                n = gi * 2 + jj
                if n < DC:
                    ub = getb()
                    P.op("act", lambda e: e.activation(out=ub[:, :], in_=pb[:, :], func=AF.Gelu), reads=[pb], writes=[ub])
                    P.dma("sp", actA.t[n * 128:(n + 1) * 128, T0:T0 + TT], ub[:, :], ub, reads=[ub], writes=[actA])
                else:
                    cch = n - DC
                    vb = getf()
                    P.op("act", lambda e: e.activation(out=vb[:, 0:TT], in_=pb[:, :], func=AF.Gelu), reads=[pb], writes=[vb])
                    stats_add(vb, vb[:, 0:TT], T0)
                    P.dma("sp", vT.t[cch * 128:(cch + 1) * 128, T0:T0 + TT], vb[:, 0:TT], vb, reads=[vb], writes=[vT])

        def spatial(p):
            stats_finalize(D)
            for tt in range(p[1]):
                T0 = p[0] + tt * TT
                for cch in range(DC):
                    vt = getf()
                    P.dma("sp", vt[:, 0:TT], vT.t[cch * 128:(cch + 1) * 128, T0:T0 + TT], vt, reads=[vT], writes=[vt])
                    vn = getb()
                    normalize_tile(vt, vt[:, 0:TT], tt * TT, alnv[:, cch:cch + 1], alnv[:, DC + cch:DC + cch + 1], vn, vn[:, :], alnv)
                    P.mm([lambda e, q=q, vn=vn: e.transpose(pt[:, q * 128:(q + 1) * 128], vn[:, q * 128:(q + 1) * 128], ident[:, :])
                          for q in range(4)], reads=[vn, ident], writes=[pt])
                    P.op("act", lambda e, cch=cch: e.activation(
                        out=vtok[:, :, cch * 128:(cch + 1) * 128], in_=pt[:, :].rearrange("p (q c) -> p q c", q=4), func=AF.Copy),
                        reads=[pt], writes=[vtok])
                for cch in range(DC):
                    g = cch // AGC
                    pmx = getpm()
                    P.mm([lambda e, q=q, cch=cch, g=g, pmx=pmx: e.matmul(
                        pmx[:, q * 128:(q + 1) * 128], vtok[:, q, cch * 128:(cch + 1) * 128], wsb[:, g * 128:(g + 1) * 128],
                        start=True, stop=True) for q in range(4)], reads=[vtok, wsb], writes=[pmx])
                    ub = getb()
                    P.dma("sp", ub[:, :], actA.t[cch * 128:(cch + 1) * 128, T0:T0 + TT], ub, reads=[actA], writes=[ub])
                    tmp = getf()
                    for q in range(4):
                        P.op("dve", lambda e, q=q, g=g: e.tensor_tensor(
                            out=tmp[:, q * 128:(q + 1) * 128], in0=pmx[:, q * 128:(q + 1) * 128],
                            in1=bsb[:, g * 128:(g + 1) * 128], op=ALU.add), reads=[pmx, bsb, tmp], writes=[tmp])
                    yb = getb()
                    P.op("dve", lambda e: e.tensor_tensor(out=yb[:, :], in0=tmp[:, 0:TT], in1=ub[:, :], op=ALU.mult),
                         reads=[tmp, ub], writes=[yb])
                    P.dma("sp", actB.t[cch * 128:(cch + 1) * 128, T0:T0 + TT], yb[:, :], yb, reads=[yb], writes=[actB])

        gemm(xin, True, 0, DC, pair_groups(win, 2 * DC), ep_in, on_pass_start=stats_reset, on_pass_end=spatial)

    def mixer_c(layer, j):
        with P.scope():
            mixer_c_body(layer, j)
        out_proj(actB, DC, W["c_w_out"].t[j], layer)

    def mixer_c_body(layer, j):
        csc = P.sb("csc%d" % j, [128, DC], F32)
        chalo = P.sb("chalo%d" % j, [128, DC, 16], F32, dma=False)
        rden = P.sb("rden", [128, 4 * TT], F32)
        P.dma("sp", rden[:, :], cst_f[:, 257:257 + 4 * TT], rden, writes=[rden])
        P.dma("sp", csc[:, :], W["c_scale"].t[j], csc, writes=[csc])
        P.op("dve", lambda e: e.memset(chalo[:, :, :], 0.0), writes=[chalo])

        def ep_pool(gi, T0, psums):
            for jj, pb in enumerate(psums):
                cch = gi * 2 + jj
                wi = cch // CGC
                if T0 < HAL:
                    P.op("dve", lambda e, pb=pb, cch=cch: e.tensor_copy(out=chalo[:, cch, :], in_=pb[:, HAL - 16:HAL]),
                         reads=[pb, chalo], writes=[chalo])
                    continue
                hb = getf()
                P.op("act", lambda e: e.activation(out=hb[:, 16:16 + TT], in_=pb[:, :], func=AF.Copy), reads=[pb], writes=[hb])
                P.op("dve", lambda e: e.tensor_copy(out=hb[:, 0:16], in_=chalo[:, cch, :]), reads=[chalo, hb], writes=[hb])
                P.op("dve", lambda e: e.tensor_copy(out=chalo[:, cch, :], in_=hb[:, TT:TT + 16]), reads=[hb, chalo], writes=[chalo])
                cur = hb
                lo = 0
                for k in range(wi + 1):
                    sh = 1 << k
                    nlo = lo + sh
                    nxt = getf()
                    P.op("dve", lambda e, cur=cur, nxt=nxt, sh=sh, nlo=nlo: e.tensor_tensor(
                        out=nxt[:, nlo:528], in0=cur[:, nlo:528], in1=cur[:, nlo - sh:528 - sh], op=ALU.add),
                        reads=[cur], writes=[nxt])
                    cur = nxt; lo = nlo
                w = float(C_WINDOWS[wi])
                pbf = getb()
                if T0 == HAL:
                    t2 = getf()
                    P.op("dve", lambda e: e.tensor_tensor(out=t2[:, 0:TT], in0=cur[:, 16:16 + TT], in1=rden[:, wi * TT:(wi + 1) * TT],
                                                          op=ALU.mult), reads=[cur, rden], writes=[t2])
                    P.op("dve", lambda e: e.tensor_tensor(out=pbf[:, :], in0=t2[:, 0:TT], in1=hb[:, 16:16 + TT], op=ALU.subtract),
                         reads=[t2, hb], writes=[pbf])
                else:
                    P.op("dve", lambda e: e.scalar_tensor_tensor(out=pbf[:, :], in0=cur[:, 16:16 + TT], scalar=1.0 / w,
                                                                 in1=hb[:, 16:16 + TT], op0=ALU.mult, op1=ALU.subtract),
                         reads=[cur, hb], writes=[pbf])
                P.dma("sp", actA.t[cch * 128:(cch + 1) * 128, T0:T0 + TT], pbf[:, :], pbf, reads=[pbf], writes=[actA])

        gemm(xin, True, 0, DC, pair_groups(W["c_w_in"].t[j], DC), ep_pool, passes=PASSES_HALO)
        for g in range(4):
            def ep_sc(gi, T0, psums, g=g):
                for jj, pb in enumerate(psums):
                    cch = g * CGC + gi * 2 + jj
                    yb = getb()
                    P.op("act", lambda e: e.activation(out=yb[:, :], in_=pb[:, :], func=AF.Copy, scale=csc[:, cch:cch + 1]),
                         reads=[pb, csc], writes=[yb])
                    P.dma("sp", actB.t[cch * 128:(cch + 1) * 128, T0:T0 + TT], yb[:, :], yb, reads=[yb], writes=[actB])
            gemm(actA, False, g * CGC * 128, CGC, pair_groups(W["c_w_group"].t[j, g], CGC), ep_sc)

    def mixer_b(layer, j):
        if STAGE in (None, "qkv"):
            with P.scope():
                mixer_b_qkv(layer, j)
        if STAGE in (None, "att"):
            with P.scope():
                mixer_b_att(layer, j)
            out_proj(actB, H, W["b_w_out"].t[j], layer)

    def mixer_b_qkv(layer, j):
        posb = P.sb("posb", [128, TT], I32)
        for tt in range(NT // TT):
            T0 = tt * TT
            P.dma("sp", posb[:, :], pos_in.t[0:1, T0:T0 + TT].broadcast_to([128, TT]), posb, writes=[posb])
            ang = getf()
            P.op("dve", lambda e: e.tensor_copy(out=ang[:, 0:TT], in_=posb[:, :]), reads=[posb], writes=[ang])
            P.op("dve", lambda e: e.tensor_scalar(out=ang[:, 0:TT], in0=ang[:, 0:TT], scalar1=invf[:, 0:1], scalar2=None,
                                                  op0=ALU.mult), reads=[ang, invf], writes=[ang])
            for which in range(2):
                a2 = getf(); nf = getf(); ni = getf()
                P.op("dve", lambda e: e.tensor_scalar(out=a2[:, 0:TT], in0=ang[:, 0:TT], scalar1=1.0,
                                                      scalar2=(math.pi / 2 if which == 0 else 0.0), op0=ALU.mult, op1=ALU.add),
                     reads=[ang], writes=[a2])
                P.op("dve", lambda e: e.tensor_scalar(out=nf[:, 0:TT], in0=a2[:, 0:TT], scalar1=float(1.0 / (2 * math.pi)),
                                                      scalar2=None, op0=ALU.mult), reads=[a2], writes=[nf])
                nii = ni.t[:, 0:TT].bitcast(I32)
                P.op("dve", lambda e: e.tensor_copy(out=nii, in_=nf[:, 0:TT]), reads=[nf], writes=[ni])
                P.op("dve", lambda e: e.tensor_copy(out=nf[:, 0:TT], in_=nii), reads=[ni], writes=[nf])
                P.op("dve", lambda e: e.scalar_tensor_tensor(out=a2[:, 0:TT], in0=nf[:, 0:TT], scalar=-6.28125,
                                                             in1=a2[:, 0:TT], op0=ALU.mult, op1=ALU.add), reads=[nf, a2], writes=[a2])
                P.op("dve", lambda e: e.scalar_tensor_tensor(out=a2[:, 0:TT], in0=nf[:, 0:TT], scalar=float(-(2 * math.pi - 6.28125)),
                                                             in1=a2[:, 0:TT], op0=ALU.mult, op1=ALU.add), reads=[nf, a2], writes=[a2])
                P.op("dve", lambda e: e.tensor_scalar(out=a2[:, 0:TT], in0=a2[:, 0:TT], scalar1=float(math.pi), scalar2=float(-math.pi),
                                                      op0=ALU.min, op1=ALU.max), reads=[a2], writes=[a2])
                P.op("act", lambda e: e.activation(out=nf[:, 0:TT], in_=a2[:, 0:TT], func=AF.Sin), reads=[a2], writes=[nf])
                P.dma("sp", csd.t[which * 128:(which + 1) * 128, T0:T0 + TT], nf[:, 0:TT], nf, reads=[nf], writes=[csd])

        win = W["b_w_in"].t[j]
        def ep_qkv(gi, T0, psums):
            for jj, pb in enumerate(psums):
                n = gi * 2 + jj
                g = n // (3 * H); kind = (n // H) % 3; h = n % H
                row = (g * H + h) * 128
                To = T0 - HAL
                if kind == 2:
                    vb = getb()
                    P.op("act", lambda e: e.activation(out=vb[:, :], in_=pb[:, :], func=AF.Copy), reads=[pb], writes=[vb])
                    P.dma("sp", v_dst.t[row:row + 128, To:To + TT], vb[:, :], vb, reads=[vb], writes=[v_dst])
                else:
                    cb = getf(); sbf = getf()
                    P.dma("sp", cb[:, 0:TT], csd.t[0:128, To:To + TT], cb, reads=[csd], writes=[cb])
                    P.dma("sp", sbf[:, 0:TT], csd.t[128:256, To:To + TT], sbf, reads=[csd], writes=[sbf])
                    qf = getf()
                    P.op("act", lambda e: e.activation(out=qf[:, 0:TT], in_=pb[:, :], func=AF.Copy), reads=[pb], writes=[qf])
                    pr = getpm()
                    P.mm([lambda e: e.matmul(pr[:, :], Rm[:, :], qf[:, 0:TT], start=True, stop=True)], reads=[Rm, qf], writes=[pr])
                    t1 = getf()
                    P.op("dve", lambda e: e.tensor_tensor(out=t1[:, 0:TT], in0=qf[:, 0:TT], in1=cb[:, 0:TT], op=ALU.mult),
                         reads=[qf, cb], writes=[t1])
                    t2 = getf()
                    P.op("dve", lambda e: e.tensor_tensor(out=t2[:, 0:TT], in0=pr[:, :], in1=sbf[:, 0:TT], op=ALU.mult),
                         reads=[pr, sbf], writes=[t2])
                    qb = getb()
                    P.op("dve", lambda e: e.tensor_tensor(out=qb[:, :], in0=t1[:, 0:TT], in1=t2[:, 0:TT], op=ALU.add),
                         reads=[t1, t2], writes=[qb])
                    dstb = q_dst if kind == 0 else k_dst
                    P.dma("sp", dstb.t[row:row + 128, To:To + TT], qb[:, :], qb, reads=[qb], writes=[dstb])

        gemm(xin, True, 0, DC, pair_groups(win, 9 * H), ep_qkv)

    def mixer_b_att(layer, j):
        P.barrier()
        KOFF = NK - NT
        if True:
            acc_o = P.view(arena.t[:, 0:NT], "acc_o", dma=False)
            acc_d = P.view(arena.t[:, NT:2 * NT], "acc_d", dma=False)
            a0 = 2 * NT; hq = NT // 2; hk = NK // 2
            qs = P.view(arena.t[:, a0:a0 + hq].bitcast(BF16), "qs")
            ks = P.view(arena.t[:, a0 + hq:a0 + hq + hk].bitcast(BF16), "ks")
            vs = P.view(arena.t[:, a0 + hq + hk:a0 + hq + 2 * hk].bitcast(BF16), "vs")
            vtk = P.view(arena.t[:, a0 + hq + 2 * hk:a0 + hq + 3 * hk].bitcast(BF16).rearrange("p (b c) -> p b c", c=128),
                         "vtk", dma=False)
            pts = [P.sb("pts%d" % i, [128, 256], BF16, dma=False) for i in range(10)]
            ob = [P.sb("ob%d" % i, [128, TT], BF16) for i in range(2)]
            tilecnt = [0]
            scale = 1.0 / math.sqrt(128.0)
            for h in range(H):
                for g, (window, dil) in enumerate(PATTERNS):
                    row = (g * H + h) * 128
                    L = NK // dil
                    nbl = L // 128
                    fq = (KOFF // dil) // 128
                    P.dma("sp", qs[:, :], q_src.t[row:row + 128, :], qs, reads=[q_src], writes=[qs])
                    P.dma("sp", ks[:, :], k_src.t[row:row + 128, :], ks, reads=[k_src], writes=[ks])
                    P.dma("sp", vs[:, :], v_src.t[row:row + 128, :], vs, reads=[v_src], writes=[vs])
                    qv = qs.t[:, :].rearrange("p (m r) -> p r m", r=dil)
                    kv = ks.t[:, :].rearrange("p (m r) -> p r m", r=dil)
                    vv = vs.t[:, :].rearrange("p (m r) -> p r m", r=dil)
                    aov = acc_o.t[:, :].rearrange("p (m r) -> p r m", r=dil)
                    adv = acc_d.t[:, :].rearrange("p (m r) -> p r m", r=dil)
                    for r in range(dil):
                        for jb0 in range(max(0, fq - 1), nbl, 4):
                            nq = min(4, nbl - jb0)
                            P.mm([lambda e, r=r, jb=jb0 + q, q=q: e.transpose(
                                pt[:, q * 128:(q + 1) * 128], vv[:, r, jb * 128:(jb + 1) * 128], ident[:, :]) for q in range(nq)],
                                reads=[vs, ident], writes=[pt])
                            b0 = r * nbl + jb0
                            P.op("act", lambda e, b0=b0, nq=nq: e.activation(
                                out=vtk[:, b0:b0 + nq, :], in_=pt[:, 0:nq * 128].rearrange("p (q c) -> p q c", q=nq), func=AF.Copy),
                                reads=[pt], writes=[vtk])
                    def phase_a(r, jt0):
                        nq = min(4, nbl - jt0)
                        tilecnt[0] += 1
                        par = tilecnt[0] % 2
                        ptile = {}
                        for kbi, kb in enumerate(range(max(0, jt0 - 1), jt0 + nq)):
                            q0 = max(kb, jt0); q1 = min(kb + 1, jt0 + nq - 1)
                            nqb = q1 - q0 + 1
                            psb = getpm()
                            P.mm([lambda e, kb=kb, q0=q0, nqb=nqb, psb=psb: e.matmul(
                                psb[:, 0:nqb * 128], kv[:, r, kb * 128:(kb + 1) * 128], qv[:, r, (q0 - fq) * 128:(q0 - fq + nqb) * 128],
                                start=True, stop=True)], reads=[ks, qs], writes=[psb])
                            pb_ = pts[par * 5 + kbi]
                            P.op("act", lambda e, psb=psb, pb_=pb_, nqb=nqb: e.activation(
                                out=pb_[:, 0:nqb * 128], in_=psb[:, 0:nqb * 128], func=AF.Exp, scale=scale),
                                reads=[psb], writes=[pb_])
                            m0 = 0 if q0 == kb else 128
                            mk = maskp if kb < fq else maskb
                            P.op("dve", lambda e, pb_=pb_, nqb=nqb, m0=m0, mk=mk: e.tensor_tensor(
                                out=pb_[:, 0:nqb * 128], in0=pb_[:, 0:nqb * 128], in1=mk[:, m0:m0 + nqb * 128], op=ALU.mult),
                                reads=[pb_, mk], writes=[pb_])
                            ptile[kb] = (pb_, q0)
                        return (r, jt0, nq, par, ptile)

                    def phase_b(state):
                        r, jt0, nq, par, ptile = state
                        po = pg[0] if par == 0 else pg[2]
                        pd = pg[1] if par == 0 else pg[3]
                        fo = []; fd = []; rd = [vtk, onesb]
                        for qi in range(nq):
                            qb_ = jt0 + qi
                            srcs = [kb for kb in (qb_ - 1, qb_) if kb >= 0]
                            for si, kb in enumerate(srcs):
                                pb_, q0 = ptile[kb]
                                off = (qb_ - q0) * 128
                                rd.append(pb_)
                                fo.append(lambda e, qi=qi, kb=kb, pb_=pb_, off=off, si=si, ns=len(srcs): e.matmul(
                                    po[:, qi * 128:(qi + 1) * 128], vtk[:, r * nbl + kb, :], pb_[:, off:off + 128],
                                    start=(si == 0), stop=(si == ns - 1)))
                                fd.append(lambda e, qi=qi, pb_=pb_, off=off, si=si, ns=len(srcs): e.matmul(
                                    pd[:, qi * 128:(qi + 1) * 128], onesb[:, :], pb_[:, off:off + 128],
                                    start=(si == 0), stop=(si == ns - 1)))
                        P.mm(fo, reads=rd, writes=[po])
                        P.mm(fd, reads=rd, writes=[pd])
                        ncol = nq * 128
                        m0 = (jt0 - fq) * 128
                        if g == 0:
                            P.op("act", lambda e: e.activation(
                                out=aov[:, r, m0:m0 + ncol], in_=po[:, 0:ncol], func=AF.Copy), reads=[po], writes=[acc_o])
                            P.op("act", lambda e: e.activation(
                                out=adv[:, r, m0:m0 + ncol], in_=pd[:, 0:ncol], func=AF.Copy), reads=[pd], writes=[acc_d])
                        else:
                            P.op("dve", lambda e: e.tensor_tensor(
                                out=aov[:, r, m0:m0 + ncol], in0=aov[:, r, m0:m0 + ncol], in1=po[:, 0:ncol], op=ALU.add),
                                reads=[po, acc_o], writes=[acc_o])
                            P.op("dve", lambda e: e.tensor_tensor(
                                out=adv[:, r, m0:m0 + ncol], in0=adv[:, r, m0:m0 + ncol], in1=pd[:, 0:ncol], op=ALU.add),
                                reads=[pd, acc_d], writes=[acc_d])

                    pending = None
                    for r in range(dil):
                        for jt0 in range(fq, nbl, 4):
                            st_ = phase_a(r, jt0)
                            if pending is not None:
                                phase_b(pending)
                            pending = st_
                    if pending is not None:
                        phase_b(pending)
                for tt in range(NT // TT):
                    T0 = tt * TT
                    rc = getf()
                    P.op("dve", lambda e: e.reciprocal(out=rc[:, 0:TT], in_=acc_d[:, T0:T0 + TT]), reads=[acc_d], writes=[rc])
                    o_ = ob[tt % 2]
                    P.op("dve", lambda e: e.tensor_tensor(out=o_[:, :], in0=acc_o[:, T0:T0 + TT], in1=rc[:, 0:TT], op=ALU.mult),
                         reads=[acc_o, rc], writes=[o_])
                    P.dma("sp", actB.t[h * 128:(h + 1) * 128, HAL + T0:HAL + T0 + TT], o_[:, :], o_, reads=[o_], writes=[actB])
            P.barrier()

    if STAGE is None:
        for cch in range(DC):
            for tt in range(SL // TT):
                T0 = tt * TT
                b = getf()
                P.dma("sp", b[:, 0:TT], xT_in.t[cch * 128:(cch + 1) * 128, T0:T0 + TT], b, writes=[b])
                P.dma("sp", xr.t[cch * 128:(cch + 1) * 128, T0:T0 + TT], b[:, 0:TT], b, reads=[b], writes=[xr])
    stop = cfg.get("STOP")

    def dump_xr():
        for cch in range(DC):
            for tt in range(NT // TT):
                T0 = HAL + tt * TT
                b = getf()
                P.dma("sp", b[:, 0:TT], xr.t[cch * 128:(cch + 1) * 128, T0:T0 + TT], b, reads=[xr], writes=[b])
                P.dma("sp", out_b.t[cch * 128:(cch + 1) * 128, T0 - HAL:T0 - HAL + TT], b[:, 0:TT], b, reads=[b], writes=[out_b])
    for layer in layers:
        kind, j = layer % 3, layer // 3
        if STAGE != "ffn":
            if kind == 0:
                mixer_a(layer, ja.index(j))
            elif kind == 1:
                mixer_b(layer, jb.index(j))
            else:
                mixer_c(layer, jc.index(j))
        if stop == (layer, "mix"):
            dump_xr(); break
        if STAGE in ("mix", "qkv", "att"):
            break
        ffn(layer, layer == layers[-1])
        if stop == (layer, "ffn") and layer != layers[-1]:
            dump_xr(); break
    P.barrier()
    es.close()
    return nc


def _pp(v, nchunk):
    return np.ascontiguousarray(np.asarray(v).reshape(nchunk, 128).T)


def host_constants(cfg, half):
    half_ = 16
    inv_freq = (np.float32(ROPE_THETA) ** (-np.arange(half_, dtype=np.float32) * np.float32(2.0) / np.float32(32))).astype(np.float32)
    cst_f = np.zeros((128, 128 + 128 + 1 + 4 * TT), np.float32)
    for i in range(16):
        cst_f[i + 16, i] = -1.0
        cst_f[i, i + 16] = 1.0
    s = np.arange(128)
    cst_f[:, 128:256] = (s[:, None] <= s[None, :]).astype(np.float32)
    cst_f[0:16, 256] = inv_freq; cst_f[16:32, 256] = inv_freq
    t = np.arange(TT, dtype=np.float32)
    for wi, w in enumerate(C_WINDOWS):
        den = np.minimum(t + 1, np.float32(w)) if half == 0 else np.full(TT, w, np.float32)
        cst_f[:, 257 + wi * TT:257 + (wi + 1) * TT] = (np.float32(1.0) / den)[None, :]
    cst_b = np.zeros((128, 640), np.float32)
    cst_b[:, 0:128] = np.eye(128)
    cst_b[:, 128:256] = (s[:, None] <= s[None, :])
    cst_b[:, 256:384] = (s[:, None] >= s[None, :])
    if half == 1:
        cst_b[:, 384:640] = cst_b[:, 128:384]
    return cst_f, cst_b.astype(ml_dtypes.bfloat16)


def declared_inputs(nc):
    names = []
    for alloc in nc.allocations:
        if isinstance(alloc, mybir.MemoryLocationSet) and alloc.kind == "ExternalInput":
            names.append(alloc.memorylocations[0].name)
    return names


def host_inputs(cfg, inp, layers, xT_list, halves=None, extra=None, pos_list=None):
    c = cfg_derive(cfg)
    D, S, DFF, H, DC, FC, AG = c["D"], c["S"], c["DFF"], c["H"], c["DC"], c["FC"], c["AG"]
    f = lambda k: np.asarray(inp[k], dtype=np.float32)
    assert len(layers) == 1 or layers == list(range(layers[0], layers[-1] + 1))
    l0, l1 = layers[0], layers[-1] + 1
    ja = [l // 3 for l in layers if l % 3 == 0]; jb = [l // 3 for l in layers if l % 3 == 1]; jc = [l // 3 for l in layers if l % 3 == 2]
    sl = lambda js: slice(js[0], js[-1] + 1)
    sh = {}
    if ja:
        sh["a_w_in"] = f("a_w_in")[sl(ja)]; sh["a_w_out"] = f("a_w_out")[sl(ja)]
        sh["a_lnv"] = np.stack([np.concatenate([_pp(f("a_ln_g")[j], DC), _pp(f("a_ln_b")[j], DC)], 1) for j in ja])
        ws = f("a_w_s")[sl(ja)]
        sh["a_wsT"] = np.ascontiguousarray(ws.transpose(0, 3, 1, 2).reshape(len(ja), 128, AG * 128))
        bs = f("a_b_s")[sl(ja)]
        sh["a_bsb"] = np.ascontiguousarray(np.broadcast_to(bs.reshape(len(ja), 1, AG * 128), (len(ja), 128, AG * 128)))
    if jb:
        sh["b_w_in"] = f("b_w_in")[sl(jb)]; sh["b_w_out"] = f("b_w_out")[sl(jb)]
    if jc:
        sh["c_w_in"] = f("c_w_in")[sl(jc)]; sh["c_w_group"] = f("c_w_group")[sl(jc)]; sh["c_w_out"] = f("c_w_out")[sl(jc)]
        sh["c_scale"] = np.stack([_pp(f("c_scale")[j], DC) for j in jc])
    sh["f_w_up"] = f("f_w_up")[l0:l1]; sh["f_w_down"] = f("f_w_down")[l0:l1]
    cw = f("f_conv_w"); cb = f("f_conv_b")
    sh["f_cv"] = np.stack([np.concatenate([_pp(cw[l, 0], 2 * FC), _pp(cw[l, 1], 2 * FC), _pp(cw[l, 2], 2 * FC), _pp(cb[l], 2 * FC)], 1)
                           for l in layers])
    sh["lnv"] = np.stack([np.concatenate([_pp(f("ln_mix_g")[l], DC), _pp(f("ln_mix_b")[l], DC),
                                          _pp(f("ln_ffn_g")[l], DC), _pp(f("ln_ffn_b")[l], DC)], 1) for l in layers])
    consts = [host_constants(cfg, 0), host_constants(cfg, 1)]
    maps = []
    for ci in range(len(xT_list)):
        m = dict(sh)
        hf = 0 if halves is None else halves[ci]
        m["cst_f"], m["cst_b"] = consts[hf]
        m["xT"] = xT_list[ci]
        if pos_list is not None:
            m["pos"] = pos_list[ci]
        if extra is not None:
            m.update(extra[ci])
        maps.append(m)
    return maps


LAUNCH_GROUPS = [[0], [1], [2], [3]]
HALO = 128


def _launch(cfg, inputs, layers, xT_list, **kw):
    nc = build(cfg, layers)
    names = declared_inputs(nc)
    maps = host_inputs(cfg, inputs, layers, xT_list, **kw)
    maps = [{k: m[k] for k in names if k in m} for m in maps]
    return run_bass_kernel_spmd(nc, maps, core_ids=list(range(len(maps)))).results


def kernel_whole_seq(inputs, cfg):
    x = np.asarray(inputs["x"], dtype=np.float32)
    pos = np.asarray(inputs["positions"], dtype=np.int32)
    xT = [np.ascontiguousarray(x[b].T) for b in range(x.shape[0])]
    pl = [np.ascontiguousarray(pos[b][None, :]) for b in range(x.shape[0])]
    for layers in cfg.get("GROUPS", LAUNCH_GROUPS):
        res = _launch(cfg, inputs, layers, xT, pos_list=pl)
        xT = [r["outT"] for r in res]
    return np.stack([np.ascontiguousarray(t.T) for t in xT]).astype(np.float32)


def kernel_split(inputs, cfg):
    x = np.asarray(inputs["x"], dtype=np.float32)
    pos = np.asarray(inputs["positions"], dtype=np.int32)
    B, S, D = x.shape
    NT = S // 2
    halves = [c % 2 for c in range(2 * B)]
    xfull = [np.ascontiguousarray(x[b].T) for b in range(B)]

    def with_halo(full):
        out = []
        for b in range(B):
            for hf in range(2):
                own = full[b][:, hf * NT:(hf + 1) * NT]
                halo = full[b][:, NT - HALO:NT] if hf == 1 else np.zeros((full[b].shape[0], HALO), full[b].dtype)
                out.append(np.ascontiguousarray(np.concatenate([halo, own], axis=1)))
        return out

    def gather(res, key="outT"):
        return [np.ascontiguousarray(np.concatenate([res[2 * b][key], res[2 * b + 1][key]], axis=1)) for b in range(B)]

    base = dict(cfg, HAL=HALO, NT=NT, NK=NT)
    for layer in range(cfg["DEPTH"]):
        kind = layer % 3
        if kind == 1:
            pl = [np.ascontiguousarray(pos[c // 2][None, (c % 2) * NT:(c % 2 + 1) * NT]) for c in range(2 * B)]
            r = _launch(dict(base, STAGE="qkv"), inputs, [layer], with_halo(xfull), halves=halves, pos_list=pl)
            extra = []
            for c in range(2 * B):
                e = {"q_in": r[c]["q_out"]}
                for nm in ("k", "v"):
                    own = r[c][nm + "_out"]
                    prev = r[c - 1][nm + "_out"] if c % 2 == 1 else np.zeros_like(own)
                    e[nm + "_in"] = np.ascontiguousarray(np.concatenate([prev, own], axis=1))
                extra.append(e)
            r = _launch(dict(base, STAGE="att", NK=2 * NT), inputs, [layer], with_halo(xfull), halves=halves, extra=extra)
        else:
            r = _launch(dict(base, STAGE="mix"), inputs, [layer], with_halo(xfull), halves=halves)
        xfull = gather(r)
        r = _launch(dict(base, STAGE="ffn"), inputs, [layer], with_halo(xfull), halves=halves)
        xfull = gather(r)
    return np.stack([np.ascontiguousarray(t.T) for t in xfull]).astype(np.float32)


def kernel(**inputs):
    cfg = CFG
    if cfg.get("MODE", "split") == "split":
        return kernel_split(inputs, cfg)
    return kernel_whole_seq(inputs, cfg)
```

```python
import math
import numpy as np
import ml_dtypes
from contextlib import ExitStack
import concourse.bass as bass
import concourse.mybir as mybir
from concourse.bass_utils import run_bass_kernel_spmd

F32 = mybir.dt.float32
BF16 = mybir.dt.bfloat16
I32 = mybir.dt.int32
AF = mybir.ActivationFunctionType
ALU = mybir.AluOpType

CFG = dict(D=4096, S=4096, DFF=11008, H=16, AG=16, DEPTH=4, NB=4)
TP = 1024
TT = 512
LN_EPS = 1e-5
ROPE_THETA = 500000.0
PATTERNS = ((128, 1), (512, 4), (2048, 16))
C_WINDOWS = (2, 4, 8, 16)


class Sem:
    _n = 0

    def __init__(self, h):
        self.h = h; self.cnt = 0; Sem._n += 1; self.uid = Sem._n


class Buf:
    __slots__ = ("t", "w", "r", "dsem", "name")

    def __init__(self, t, name, dsem=None):
        self.t = t; self.w = {}; self.r = {}; self.dsem = dsem; self.name = name

    def __getitem__(self, k):
        return self.t[k]


class Prog:
    def __init__(self, nc, es):
        self.nc = nc; self.es = es
        self.eng = {"pe": nc.tensor, "act": nc.scalar, "dve": nc.vector, "pool": nc.gpsimd, "sp": nc.sync}
        self.main_es = es
        self.psem = {}
        for e in ("pe", "act", "dve", "pool"):
            self.psem[e] = Sem(es.enter_context(nc.semaphore("prog_" + e)))
        self.waited = {e: {} for e in self.eng}
        self.nsem = 4
        self.allbufs = []
        self.sempool = []
        self.allsems = []

    def get_dsem(self):
        if self.sempool:
            return self.sempool.pop()
        self.nsem += 1
        sm = Sem(self.main_es.enter_context(self.nc.semaphore("dsem%d" % self.nsem)))
        self.allsems.append(sm)
        return sm

    def scope(self):
        P = self

        class _S:
            def __enter__(s2):
                s2.old = P.es; s2.es = ExitStack(); P.es = s2.es; s2.n = len(P.allbufs); return s2

            def __exit__(s2, *a):
                P.barrier()
                for b in P.allbufs[s2.n:]:
                    if b.dsem is not None:
                        P.sempool.append(b.dsem)
                del P.allbufs[s2.n:]
                s2.es.close(); P.es = s2.old
                return False
        return _S()

    def sb(self, name, shape, dt, dma=True):
        t = self.es.enter_context(self.nc.sbuf_tensor(name, list(shape), dt))
        ds = self.get_dsem() if dma else None
        b = Buf(t, name, ds); self.allbufs.append(b); return b

    def view(self, ap, name, dma=True):
        b = Buf(ap, name, self.get_dsem() if dma else None); self.allbufs.append(b); return b

    def ps(self, name, shape, dt):
        t = self.es.enter_context(self.nc.psum_tensor(name, list(shape), dt))
        b = Buf(t, name); self.allbufs.append(b); return b

    def dram(self, name, shape, dt):
        t = self.nc.dram_tensor(name, list(shape), dt)
        b = Buf(t.ap(), name); self.allbufs.append(b); return b

    def dram_in(self, name, shape, dt):
        t = self.nc.dram_tensor(name, list(shape), dt, kind="ExternalInput")
        return Buf(t.ap(), name)

    def wait(self, e, tok):
        sem, val = tok
        k = sem.uid
        if self.waited[e].get(k, 0) >= val:
            return
        self.eng[e].wait_ge(sem.h, val)
        self.waited[e][k] = val

    def deps(self, e, reads, writes):
        for b in reads:
            for t in b.w.values():
                self.wait(e, t)
        for b in writes:
            for t in b.w.values():
                self.wait(e, t)
            for t in b.r.values():
                self.wait(e, t)

    def commit(self, tok, reads, writes):
        k = tok[0].uid
        for b in reads:
            b.r[k] = tok
        for b in writes:
            b.w = {k: tok}; b.r = {}

    def op(self, e, fn, reads=(), writes=()):
        self.deps(e, reads, writes)
        ins = fn(self.eng[e])
        sm = self.psem[e]; sm.cnt += 1
        ins.then_inc(sm.h, 1)
        tok = (sm, sm.cnt)
        self.commit(tok, reads, writes)
        return tok

    def mm(self, fns, reads, writes):
        self.deps("pe", reads, writes)
        ins = None
        for f in fns:
            ins = f(self.nc.tensor)
        sm = self.psem["pe"]; sm.cnt += 1
        ins.then_inc(sm.h, 1)
        tok = (sm, sm.cnt)
        self.commit(tok, reads, writes)
        return tok

    def dma(self, q, out_ap, in_ap, sbuf, reads=(), writes=(), **kw):
        self.deps(q, reads, writes)
        ins = self.eng[q].dma_start(out=out_ap, in_=in_ap, **kw)
        sm = sbuf.dsem; sm.cnt += 16
        ins.then_inc(sm.h, 16)
        tok = (sm, sm.cnt)
        self.commit(tok, reads, writes)
        return tok

    def barrier(self):
        toks = []
        for e in ("pe", "act", "dve", "pool"):
            if self.psem[e].cnt:
                toks.append((self.psem[e], self.psem[e].cnt))
        for sm in self.allsems:
            if sm.cnt:
                toks.append((sm, sm.cnt))
        for e in self.eng:
            for t in toks:
                self.wait(e, t)


def cfg_derive(c):
    c = dict(c)
    c["DC"] = c["D"] // 128
    c["FC"] = c["DFF"] // 128
    c["NQKV"] = 3 * 3 * c["H"]
    c["AGC"] = (c["D"] // c["AG"]) // 128
    c["CGC"] = (c["D"] // 4) // 128
    c["NPASS"] = c["S"] // TP
    c["NA"] = len(range(0, c["DEPTH"], 3)); c["NBm"] = len(range(1, c["DEPTH"], 3)); c["NCm"] = len(range(2, c["DEPTH"], 3))
    return c


def build(cfg, layers=None):
    c = cfg_derive(cfg)
    D, S, DFF, H, DC, FC = c["D"], c["S"], c["DFF"], c["H"], c["DC"], c["FC"]
    AG, AGC, CGC, NPASS, DEPTH = c["AG"], c["AGC"], c["CGC"], c["NPASS"], c["DEPTH"]
    NA, NBm, NCm = c["NA"], c["NBm"], c["NCm"]
    layers = list(range(DEPTH)) if layers is None else list(layers)
    STAGE = cfg.get("STAGE")
    HAL = cfg.get("HAL", 0)
    NT = cfg.get("NT", S)
    SL = HAL + NT
    NK = cfg.get("NK", NT)
    PASSES_OWN = [(HAL + i * TP, [TT] * (TP // TT)) for i in range(NT // TP)]
    assert HAL == 0 or HAL <= TT
    PASSES_HALO = ([(0, [HAL] + PASSES_OWN[0][1])] + PASSES_OWN[1:]) if HAL else PASSES_OWN
    XSW = TP + (HAL if (STAGE == "ffn" or (STAGE == "mix" and any(l % 3 == 2 for l in layers))) else 0)
    cur = {"T0p": 0}
    NL = len(layers)
    ja = [l // 3 for l in layers if l % 3 == 0]; jb = [l // 3 for l in layers if l % 3 == 1]; jc = [l // 3 for l in layers if l % 3 == 2]
    NPARTS = 3 if FC > 2 * DC else 2
    KCMAX = max(DC, -(-FC // NPARTS), H)
    alpha = float((2 * DEPTH) ** 0.25)
    nc = bass.Bass("TRN2", target_bir_lowering=False)
    es = ExitStack()
    P = Prog(nc, es)

    xT_in = P.dram_in("xT", [D, SL], F32)
    pos_in = P.dram_in("pos", [1, NT], I32) if (jb and STAGE in (None, "qkv")) else None
    W = {}
    need_mix = STAGE in (None, "mix", "qkv", "att")
    need_ffn = STAGE in (None, "ffn")
    if ja and need_mix:
        W["a_w_in"] = P.dram_in("a_w_in", [len(ja), D, 2 * D], F32)
        W["a_w_out"] = P.dram_in("a_w_out", [len(ja), D, D], F32)
        W["a_lnv"] = P.dram_in("a_lnv", [len(ja), 128, 2 * DC], F32)
        W["a_wsT"] = P.dram_in("a_wsT", [len(ja), 128, AG * 128], F32)
        W["a_bsb"] = P.dram_in("a_bsb", [len(ja), 128, AG * 128], F32)
    if jb and STAGE in (None, "qkv"):
        W["b_w_in"] = P.dram_in("b_w_in", [len(jb), D, 9 * H * 128], F32)
    if jb and STAGE in (None, "att"):
        W["b_w_out"] = P.dram_in("b_w_out", [len(jb), H * 128, D], F32)
    if jc and need_mix:
        W["c_w_in"] = P.dram_in("c_w_in", [len(jc), D, D], F32)
        W["c_w_group"] = P.dram_in("c_w_group", [len(jc), 4, D // 4, D // 4], F32)
        W["c_w_out"] = P.dram_in("c_w_out", [len(jc), D, D], F32)
        W["c_scale"] = P.dram_in("c_scale", [len(jc), 128, DC], F32)
    if need_ffn:
        W["f_w_up"] = P.dram_in("f_w_up", [NL, D, 2 * DFF], F32)
        W["f_w_down"] = P.dram_in("f_w_down", [NL, DFF, D], F32)
        W["f_cv"] = P.dram_in("f_cv", [NL, 128, 8 * FC], F32)
    if STAGE != "qkv":
        W["lnv"] = P.dram_in("lnv", [NL, 128, 4 * DC], F32)
    cst_f = P.dram_in("cst_f", [128, 128 + 128 + 1 + 4 * TT], F32)
    cst_b = P.dram_in("cst_b", [128, 128 + 256 + 256], BF16)
    if STAGE != "qkv":
        out_t = nc.dram_tensor("outT", [D, NT], F32, kind="ExternalOutput")
        out_b = Buf(out_t.ap(), "outT")
    else:
        out_b = None

    xr = P.dram("xr", [D, SL], F32)
    actA = P.dram("actA", [max(D, DFF), SL], BF16)
    actB = P.dram("actB", [D, SL], BF16)
    vT = P.dram("vT", [D, SL], F32)
    QR = 3 * H * 128
    if STAGE == "qkv":
        q_dst = Buf(nc.dram_tensor("q_out", [QR, NT], BF16, kind="ExternalOutput").ap(), "q_out")
        k_dst = Buf(nc.dram_tensor("k_out", [QR, NT], BF16, kind="ExternalOutput").ap(), "k_out")
        v_dst = Buf(nc.dram_tensor("v_out", [QR, NT], BF16, kind="ExternalOutput").ap(), "v_out")
        q_src = k_src = v_src = None
    elif STAGE == "att":
        q_src = P.dram_in("q_in", [QR, NT], BF16)
        k_src = P.dram_in("k_in", [QR, NK], BF16)
        v_src = P.dram_in("v_in", [QR, NK], BF16)
        q_dst = k_dst = v_dst = None
    else:
        assert NK == NT
        q_dst = q_src = P.dram("qT", [QR, NT], BF16)
        k_dst = k_src = P.dram("kT", [QR, NT], BF16)
        v_dst = v_src = P.dram("vvT", [QR, NT], BF16)
    csd = P.dram("csd", [2 * 128, NT], F32)

    ident = P.sb("ident", [128, 128], BF16)
    maskb = P.sb("maskb", [128, 256], BF16)
    maskp = P.sb("maskp", [128, 256], BF16)
    Rm = P.sb("Rm", [128, 128], F32)
    tril = P.sb("tril", [128, 128], F32)
    invf = P.sb("invf", [128, 1], F32)
    ones = P.sb("ones", [128, 128], F32, dma=False)
    onesb = P.sb("onesb", [128, 128], BF16, dma=False)
    s1 = P.sb("s1", [128, TP], F32, dma=False)
    s2 = P.sb("s2", [128, TP], F32, dma=False)
    lnv = P.sb("lnv_s", [128, 4 * DC], F32)
    ARENA = max(KCMAX * XSW // 2, 2 * NT + NT // 2 + 3 * (NK // 2))
    arena = P.sb("arena", [128, ARENA], F32, dma=False)
    xs = P.view(arena.t[:, 0:KCMAX * XSW // 2].bitcast(BF16).rearrange("p (c t) -> p c t", t=XSW), "xs")
    WBW = cfg.get("WBW", 512 if (STAGE in ("ffn", "qkv", "att") or (STAGE == "mix" and not ja)) else 256)
    wb = [P.sb("wb%d" % i, [128, KCMAX, WBW], BF16) for i in range(2)]
    NSCR = 12
    scr = [P.sb("scr%d" % i, [128, 528], F32) for i in range(NSCR)]
    NSB = 4
    scb = [P.sb("scb%d" % i, [128, 512], BF16) for i in range(NSB)]
    pg = [P.ps("pg%d" % i, [128, TT], F32) for i in range(4)]
    pm = [P.ps("pm%d" % i, [128, TT], F32) for i in range(3)]
    pt = P.ps("ptb", [128, TT], BF16)

    P.dma("sp", ident[:, :], cst_b[:, 0:128], ident, writes=[ident])
    P.dma("sp", maskb[:, :], cst_b[:, 128:384], maskb, writes=[maskb])
    P.dma("sp", maskp[:, :], cst_b[:, 384:640], maskp, writes=[maskp])
    P.dma("sp", Rm[:, :], cst_f[:, 0:128], Rm, writes=[Rm])
    P.dma("sp", tril[:, :], cst_f[:, 128:256], tril, writes=[tril])
    P.dma("sp", invf[:, :], cst_f[:, 256:257], invf, writes=[invf], allow_slow_non_contiguous=True)
    P.op("dve", lambda e: e.memset(ones[:, :], 1.0), writes=[ones])
    P.op("dve", lambda e: e.memset(onesb[:, :], 1.0), writes=[onesb])

    rr = {"f": 0, "b": 0, "g": 0, "m": 0}

    def getf():
        rr["f"] = (rr["f"] + 1) % NSCR; return scr[rr["f"]]

    def getb():
        rr["b"] = (rr["b"] + 1) % NSB; return scb[rr["b"]]

    def getpm():
        rr["m"] = (rr["m"] + 1) % 3; return pm[rr["m"]]

    xin = xr if STAGE is None else xT_in

    def gemm(src, src_f32, K0, KC, groups, ep, pre=None, on_pass_start=None, on_pass_end=None, passes=None):
        srcv = src.t[K0:K0 + KC * 128, :].rearrange("(c p) t -> p c t", p=128)
        for p in (passes if passes is not None else PASSES_OWN):
            T0p, widths = p
            cur["T0p"] = T0p
            ptot = sum(widths)
            assert ptot <= XSW
            toffs = [sum(widths[:i]) for i in range(len(widths))]
            q = "pool" if src_f32 else "sp"
            step = 8
            for c0 in range(0, KC, step):
                c1 = min(KC, c0 + step)
                P.dma(q, xs[:, c0:c1, 0:ptot], srcv[:, c0:c1, T0p:T0p + ptot], xs, reads=[src], writes=[xs])
            if on_pass_start:
                on_pass_start(p)

            def loadw(sgi):
                slot = wb[sgi % 2]
                for wap, off in groups[sgi]["loads"]:
                    wv = wap.rearrange("(c p) n -> p c n", p=128)
                    P.dma("pool", slot[:, 0:KC, off:off + wap.shape[1]], wv, slot, writes=[slot])
            loadw(0)
            units = []
            for sgi in range(len(groups)):
                for u in groups[sgi]["units"]:
                    units.append((sgi, u))
            for gi, (sgi, unit) in enumerate(units):
                if (gi == 0 or units[gi - 1][0] != sgi) and sgi + 1 < len(groups):
                    loadw(sgi + 1)
                slot = wb[sgi % 2]
                CH = len(unit)
                for tt in range(len(widths)):
                    tw = widths[tt]; xo = toffs[tt]
                    T0 = T0p + xo
                    if pre:
                        pre(gi, T0)
                    psums = []
                    for j in range(CH):
                        woff = unit[j]
                        rr["g"] = (rr["g"] + 1) % 4
                        pb = pg[rr["g"]]
                        psums.append(pb)
                        fns = []
                        if True:
                            for k in range(KC):
                                fns.append(lambda e, k=k, woff=woff, pb=pb, xo=xo, slot=slot, tw=tw: e.matmul(
                                    pb[:, 0:tw], slot[:, k, woff:woff + 128], xs[:, k, xo:xo + tw],
                                    start=(k == 0), stop=(k == KC - 1)))
                        P.mm(fns, reads=[slot, xs], writes=[pb])
                    ep(gi, T0, psums)
            if on_pass_end:
                on_pass_end(p)

    def wcols(wap2d, n0, K0=0, KC=None):
        KC = KC if KC is not None else wap2d.shape[0] // 128
        return wap2d[K0:K0 + KC * 128, n0:n0 + 128]

    def pair_groups(wap2d, nchunks, K0=0, KC=None):
        KCx = KC if KC is not None else wap2d.shape[0] // 128
        g = []
        per = WBW // 128
        for n in range(0, nchunks, per):
            w = min(per, nchunks - n)
            loads = [(wap2d[K0:K0 + KCx * 128, n * 128:(n + w) * 128], 0)]
            units = [[(u + j) * 128 for j in range(min(2, w - u))] for u in range(0, w, 2)]
            g.append(dict(loads=loads, units=units))
        return g

    def stats_reset(p):
        P.op("dve", lambda e: e.memset(s1[:, :], 0.0), writes=[s1])
        P.op("dve", lambda e: e.memset(s2[:, :], 0.0), writes=[s2])

    def stats_add(zb, zap, T0):
        c0 = T0 - cur["T0p"]
        sq = getf()
        P.op("act", lambda e: e.activation(out=sq[:, 0:TT], in_=zap, func=AF.Square), reads=[zb], writes=[sq])
        P.op("dve", lambda e: e.tensor_tensor(out=s1[:, c0:c0 + TT], in0=s1[:, c0:c0 + TT], in1=zap, op=ALU.add),
             reads=[zb, s1], writes=[s1])
        P.op("dve", lambda e: e.tensor_tensor(out=s2[:, c0:c0 + TT], in0=s2[:, c0:c0 + TT], in1=sq[:, 0:TT], op=ALU.add),
             reads=[sq, s2], writes=[s2])

    def stats_finalize(nfeat):
        inv = 1.0 / nfeat
        for tt in range(TP // TT):
            sl = slice(tt * TT, (tt + 1) * TT)
            pa = getpm(); pbv = getpm()
            P.mm([lambda e, pa=pa, sl=sl: e.matmul(pa[:, :], ones[:, :], s1[:, sl], start=True, stop=True)],
                 reads=[ones, s1], writes=[pa])
            P.mm([lambda e, pbv=pbv, sl=sl: e.matmul(pbv[:, :], ones[:, :], s2[:, sl], start=True, stop=True)],
                 reads=[ones, s2], writes=[pbv])
            mean = getf(); tmp = getf()
            P.op("dve", lambda e: e.tensor_scalar(out=mean[:, 0:TT], in0=pa[:, :], scalar1=inv, scalar2=None, op0=ALU.mult),
                 reads=[pa], writes=[mean])
            P.op("dve", lambda e: e.tensor_tensor(out=tmp[:, 0:TT], in0=mean[:, 0:TT], in1=mean[:, 0:TT], op=ALU.mult),
                 reads=[mean], writes=[tmp])
            P.op("dve", lambda e: e.scalar_tensor_tensor(out=tmp[:, 0:TT], in0=pbv[:, :], scalar=inv, in1=tmp[:, 0:TT],
                                                         op0=ALU.mult, op1=ALU.subtract), reads=[pbv, tmp], writes=[tmp])
            P.op("dve", lambda e: e.tensor_scalar(out=tmp[:, 0:TT], in0=tmp[:, 0:TT], scalar1=1.0, scalar2=LN_EPS,
                                                  op0=ALU.mult, op1=ALU.add), reads=[tmp], writes=[tmp])
            P.op("act", lambda e: e.activation(out=tmp[:, 0:TT], in_=tmp[:, 0:TT], func=AF.Sqrt), reads=[tmp], writes=[tmp])
            P.op("dve", lambda e: e.reciprocal(out=s2[:, sl], in_=tmp[:, 0:TT]), reads=[tmp, s2], writes=[s2])
            P.op("dve", lambda e: e.scalar_tensor_tensor(out=s1[:, sl], in0=mean[:, 0:TT], scalar=-1.0, in1=s2[:, sl],
                                                         op0=ALU.mult, op1=ALU.mult), reads=[mean, s2, s1], writes=[s1])

    def normalize_tile(zb, zap, c0, gap, bap, outb, outap, gbbuf=None):
        t = getf()
        P.op("dve", lambda e: e.tensor_tensor(out=t[:, 0:TT], in0=zap, in1=s2[:, c0:c0 + TT], op=ALU.mult),
             reads=[zb, s2], writes=[t])
        P.op("dve", lambda e: e.tensor_tensor(out=t[:, 0:TT], in0=t[:, 0:TT], in1=s1[:, c0:c0 + TT], op=ALU.add),
             reads=[t, s1], writes=[t])
        P.op("act", lambda e: e.activation(out=outap, in_=t[:, 0:TT], func=AF.Identity, scale=gap, bias=bap),
             reads=[t, gbbuf if gbbuf is not None else lnv], writes=[outb])

    def ln_sweep(p, goff, boff, dst):
        stats_finalize(D)
        for cch in range(DC):
            for tt in range(len(p[1])):
                T0 = p[0] + tt * TT
                zt = getf()
                P.dma("sp", zt[:, 0:TT], xr.t[cch * 128:(cch + 1) * 128, T0:T0 + TT], zt, reads=[xr], writes=[zt])
                xn = getf()
                normalize_tile(zt, zt[:, 0:TT], tt * TT, lnv[:, goff + cch:goff + cch + 1],
                               lnv[:, boff + cch:boff + cch + 1], xn, xn[:, 0:TT])
                Td = T0 - HAL if dst is out_b else T0
                P.dma("sp", dst.t[cch * 128:(cch + 1) * 128, Td:Td + TT], xn[:, 0:TT], xn, reads=[xn], writes=[dst])

    def make_residual(scal, with_stats, rsrc=None):
        st = {}
        rsrc = xr if rsrc is None else rsrc

        def pre(gi, T0):
            tiles = []
            for j in range(2):
                cch = gi * 2 + j
                if cch >= DC:
                    break
                xin_t = getf()
                P.dma("sp", xin_t[:, 0:TT], rsrc.t[cch * 128:(cch + 1) * 128, T0:T0 + TT], xin_t, reads=[rsrc], writes=[xin_t])
                tiles.append(xin_t)
            st[(gi, T0)] = tiles

        def ep(gi, T0, psums):
            tiles = st.pop((gi, T0))
            for j, pb in enumerate(psums):
                cch = gi * 2 + j
                xin_t = tiles[j]
                z = getf()
                P.op("dve", lambda e: e.scalar_tensor_tensor(out=z[:, 0:TT], in0=xin_t[:, 0:TT], scalar=scal, in1=pb[:, :],
                                                             op0=ALU.mult, op1=ALU.add), reads=[xin_t, pb], writes=[z])
                if with_stats:
                    stats_add(z, z[:, 0:TT], T0)
                P.dma("sp", xr.t[cch * 128:(cch + 1) * 128, T0:T0 + TT], z[:, 0:TT], z, reads=[z], writes=[xr])
        return pre, ep

    def ffn(layer, is_last):
        li = layers.index(layer)
        if not hasattr(ffn, "cv"):
            ffn.cv = P.sb("cv", [128, 8 * FC], F32)
            ffn.halo = P.sb("fhalo", [128, 2 * FC, 2], F32, dma=False)
        cv = ffn.cv; halo = ffn.halo
        P.dma("sp", cv[:, :], W["f_cv"].t[li], cv, writes=[cv])
        P.dma("sp", lnv[:, :], W["lnv"].t[li], lnv, writes=[lnv])
        wup = W["f_w_up"].t[li]
        groups = []
        per = WBW // 256
        for i in range(0, FC, per):
            w = min(per, FC - i)
            groups.append(dict(loads=[(wup[:, i * 128:(i + w) * 128], 0), (wup[:, DFF + i * 128:DFF + (i + w) * 128], w * 128)],
                               units=[[u * 128, (w + u) * 128] for u in range(w)]))
        P.op("dve", lambda e: e.memset(halo[:, :, :], 0.0), writes=[halo])

        def conv(pb, ch, T0):
            hb = getf()
            P.op("act", lambda e: e.activation(out=hb[:, 2:2 + TT], in_=pb[:, :], func=AF.Copy), reads=[pb], writes=[hb])
            P.op("dve", lambda e: e.tensor_copy(out=hb[:, 0:2], in_=halo[:, ch, :]), reads=[halo, hb], writes=[hb])
            P.op("dve", lambda e: e.tensor_copy(out=halo[:, ch, :], in_=hb[:, TT:TT + 2]), reads=[hb, halo], writes=[halo])
            acc = getf()
            w = lambda j: cv[:, j * 2 * FC + ch:j * 2 * FC + ch + 1]
            P.op("act", lambda e: e.activation(out=acc[:, 0:TT], in_=hb[:, 2:2 + TT], func=AF.Identity, scale=w(2), bias=w(3)),
                 reads=[hb, cv], writes=[acc])
            P.op("dve", lambda e: e.scalar_tensor_tensor(out=acc[:, 0:TT], in0=hb[:, 1:1 + TT], scalar=w(1), in1=acc[:, 0:TT],
                                                         op0=ALU.mult, op1=ALU.add), reads=[hb, cv, acc], writes=[acc])
            P.op("dve", lambda e: e.scalar_tensor_tensor(out=acc[:, 0:TT], in0=hb[:, 0:TT], scalar=w(0), in1=acc[:, 0:TT],
                                                         op0=ALU.mult, op1=ALU.add), reads=[hb, cv, acc], writes=[acc])
            return acc

        def ep_up(gi, T0, psums):
            if T0 < HAL:
                for pb_, ch in ((psums[0], gi), (psums[1], FC + gi)):
                    P.op("dve", lambda e, pb_=pb_, ch=ch: e.tensor_copy(out=halo[:, ch, :], in_=pb_[:, HAL - 2:HAL]),
                         reads=[pb_, halo], writes=[halo])
                return
            cg = conv(psums[0], gi, T0)
            cu = conv(psums[1], FC + gi, T0)
            sg = getf()
            P.op("act", lambda e: e.activation(out=sg[:, 0:TT], in_=cg[:, 0:TT], func=AF.Silu), reads=[cg], writes=[sg])
            ab = getb()
            P.op("dve", lambda e: e.tensor_tensor(out=ab[:, :], in0=sg[:, 0:TT], in1=cu[:, 0:TT], op=ALU.mult),
                 reads=[sg, cu], writes=[ab])
            P.dma("sp", actA.t[gi * 128:(gi + 1) * 128, T0:T0 + TT], ab[:, :], ab, reads=[ab], writes=[actA])

        gemm(xin, True, 0, DC, groups, ep_up, passes=PASSES_HALO)
        wdn = W["f_w_down"].t[li]
        dst = out_b if is_last else xr
        per = -(-FC // NPARTS)
        k0 = 0
        for part in range(NPARTS):
            kc = min(per, FC - k0)
            last = part == NPARTS - 1
            prex, epx = make_residual(alpha if part == 0 else 1.0, last, rsrc=(xin if part == 0 else xr))
            gemm(actA, False, k0 * 128, kc, pair_groups(wdn, DC, k0 * 128, kc), epx, pre=prex,
                 on_pass_start=stats_reset if last else None,
                 on_pass_end=(lambda p: ln_sweep(p, 2 * DC, 3 * DC, dst)) if last else None)
            k0 += kc

    def out_proj(src, KC, wap, layer):
        P.dma("sp", lnv[:, :], W["lnv"].t[layers.index(layer)], lnv, writes=[lnv])
        pre, ep = make_residual(alpha, True, rsrc=xin)
        gemm(src, False, 0, KC, pair_groups(wap, DC, 0, KC), ep, pre=pre,
             on_pass_start=stats_reset,
             on_pass_end=lambda p: ln_sweep(p, 0, DC, out_b if STAGE in ("mix", "att") else xr))

    def mixer_a(layer, j):
        with P.scope():
            mixer_a_body(layer, j)
        out_proj(actB, DC, W["a_w_out"].t[j], layer)

    def mixer_a_body(layer, j):
        alnv = P.sb("alnv%d" % j, [128, 2 * DC], F32)
        wsb = P.sb("wsb%d" % j, [128, AG * 128], BF16, dma=False)
        bsb = P.sb("bsb%d" % j, [128, AG * 128], F32)
        vtok = P.sb("vtok%d" % j, [128, 4, D], BF16, dma=False)
        P.dma("sp", alnv[:, :], W["a_lnv"].t[j], alnv, writes=[alnv])
        P.dma("sp", bsb[:, :], W["a_bsb"].t[j], bsb, writes=[bsb])
        for g in range(AG):
            wsf = getf()
            P.dma("sp", wsf[:, 0:128], W["a_wsT"].t[j, :, g * 128:(g + 1) * 128], wsf, writes=[wsf])
            P.op("dve", lambda e, g=g, wsf=wsf: e.tensor_tensor(out=wsb[:, g * 128:(g + 1) * 128], in0=wsf[:, 0:128],
                                                                 in1=tril[:, :], op=ALU.mult), reads=[wsf, tril], writes=[wsb])
        win = W["a_w_in"].t[j]

        def ep_in(gi, T0, psums):
            for jj, pb in enumerate(psums):
                n = gi * 2 + jj
                if n < DC:
                    ub = getb()
                    P.op("act", lambda e: e.activation(out=ub[:, :], in_=pb[:, :], func=AF.Gelu), reads=[pb], writes=[ub])
                    P.dma("sp", actA.t[n * 128:(n + 1) * 128, T0:T0 + TT], ub[:, :], ub, reads=[ub], writes=[actA])
                else:
                    cch = n - DC
                    vb = getf()
                    P.op("act", lambda e: e.activation(out=vb[:, 0:TT], in_=pb[:, :], func=AF.Gelu), reads=[pb], writes=[vb])
                    stats_add(vb, vb[:, 0:TT], T0)
                    P.dma("sp", vT.t[cch * 128:(cch + 1) * 128, T0:T0 + TT], vb[:, 0:TT], vb, reads=[vb], writes=[vT])

        def spatial(p):
            stats_finalize(D)
            for tt in range(len(p[1])):
                T0 = p[0] + tt * TT
                for cch in range(DC):
                    vt = getf()
                    P.dma("sp", vt[:, 0:TT], vT.t[cch * 128:(cch + 1) * 128, T0:T0 + TT], vt, reads=[vT], writes=[vt])
                    vn = getb()
                    normalize_tile(vt, vt[:, 0:TT], tt * TT, alnv[:, cch:cch + 1], alnv[:, DC + cch:DC + cch + 1], vn, vn[:, :], alnv)
                    P.mm([lambda e, q=q, vn=vn: e.transpose(pt[:, q * 128:(q + 1) * 128], vn[:, q * 128:(q + 1) * 128], ident[:, :])
                          for q in range(4)], reads=[vn, ident], writes=[pt])
                    P.op("act", lambda e, cch=cch: e.activation(
                        out=vtok[:, :, cch * 128:(cch + 1) * 128], in_=pt[:, :].rearrange("p (q c) -> p q c", q=4), func=AF.Copy),
                        reads=[pt], writes=[vtok])
                for cch in range(DC):
                    g = cch // AGC
                    pmx = getpm()
                    P.mm([lambda e, q=q, cch=cch, g=g, pmx=pmx: e.matmul(
                        pmx[:, q * 128:(q + 1) * 128], vtok[:, q, cch * 128:(cch + 1) * 128], wsb[:, g * 128:(g + 1) * 128],
                        start=True, stop=True) for q in range(4)], reads=[vtok, wsb], writes=[pmx])
                    ub = getb()
                    P.dma("sp", ub[:, :], actA.t[cch * 128:(cch + 1) * 128, T0:T0 + TT], ub, reads=[actA], writes=[ub])
                    tmp = getf()
                    for q in range(4):
                        P.op("dve", lambda e, q=q, g=g: e.tensor_tensor(
                            out=tmp[:, q * 128:(q + 1) * 128], in0=pmx[:, q * 128:(q + 1) * 128],
                            in1=bsb[:, g * 128:(g + 1) * 128], op=ALU.add), reads=[pmx, bsb, tmp], writes=[tmp])
                    yb = getb()
                    P.op("dve", lambda e: e.tensor_tensor(out=yb[:, :], in0=tmp[:, 0:TT], in1=ub[:, :], op=ALU.mult),
                         reads=[tmp, ub], writes=[yb])
                    P.dma("sp", actB.t[cch * 128:(cch + 1) * 128, T0:T0 + TT], yb[:, :], yb, reads=[yb], writes=[actB])

        gemm(xin, True, 0, DC, pair_groups(win, 2 * DC), ep_in, on_pass_start=stats_reset, on_pass_end=spatial)

    def mixer_c(layer, j):
        with P.scope():
            mixer_c_body(layer, j)
        out_proj(actB, DC, W["c_w_out"].t[j], layer)

    def mixer_c_body(layer, j):
        csc = P.sb("csc%d" % j, [128, DC], F32)
        chalo = P.sb("chalo%d" % j, [128, DC, 16], F32, dma=False)
        rden = P.sb("rden", [128, 4 * TT], F32)
        P.dma("sp", rden[:, :], cst_f[:, 257:257 + 4 * TT], rden, writes=[rden])
        P.dma("sp", csc[:, :], W["c_scale"].t[j], csc, writes=[csc])
        P.op("dve", lambda e: e.memset(chalo[:, :, :], 0.0), writes=[chalo])

        def ep_pool(gi, T0, psums):
            for jj, pb in enumerate(psums):
                cch = gi * 2 + jj
                wi = cch // CGC
                if T0 < HAL:
                    P.op("dve", lambda e, pb=pb, cch=cch: e.tensor_copy(out=chalo[:, cch, :], in_=pb[:, HAL - 16:HAL]),
                         reads=[pb, chalo], writes=[chalo])
                    continue
                hb = getf()
                P.op("act", lambda e: e.activation(out=hb[:, 16:16 + TT], in_=pb[:, :], func=AF.Copy), reads=[pb], writes=[hb])
                P.op("dve", lambda e: e.tensor_copy(out=hb[:, 0:16], in_=chalo[:, cch, :]), reads=[chalo, hb], writes=[hb])
                P.op("dve", lambda e: e.tensor_copy(out=chalo[:, cch, :], in_=hb[:, TT:TT + 16]), reads=[hb, chalo], writes=[chalo])
                cur = hb
                lo = 0
                for k in range(wi + 1):
                    sh = 1 << k
                    nlo = lo + sh
                    nxt = getf()
                    P.op("dve", lambda e, cur=cur, nxt=nxt, sh=sh, nlo=nlo: e.tensor_tensor(
                        out=nxt[:, nlo:528], in0=cur[:, nlo:528], in1=cur[:, nlo - sh:528 - sh], op=ALU.add),
                        reads=[cur], writes=[nxt])
                    cur = nxt; lo = nlo
                w = float(C_WINDOWS[wi])
                pbf = getb()
                if T0 == HAL:
                    t2 = getf()
                    P.op("dve", lambda e: e.tensor_tensor(out=t2[:, 0:TT], in0=cur[:, 16:16 + TT], in1=rden[:, wi * TT:(wi + 1) * TT],
                                                          op=ALU.mult), reads=[cur, rden], writes=[t2])
                    P.op("dve", lambda e: e.tensor_tensor(out=pbf[:, :], in0=t2[:, 0:TT], in1=hb[:, 16:16 + TT], op=ALU.subtract),
                         reads=[t2, hb], writes=[pbf])
                else:
                    P.op("dve", lambda e: e.scalar_tensor_tensor(out=pbf[:, :], in0=cur[:, 16:16 + TT], scalar=1.0 / w,
                                                                 in1=hb[:, 16:16 + TT], op0=ALU.mult, op1=ALU.subtract),
                         reads=[cur, hb], writes=[pbf])
                P.dma("sp", actA.t[cch * 128:(cch + 1) * 128, T0:T0 + TT], pbf[:, :], pbf, reads=[pbf], writes=[actA])

        gemm(xin, True, 0, DC, pair_groups(W["c_w_in"].t[j], DC), ep_pool, passes=PASSES_HALO)
        for g in range(4):
            def ep_sc(gi, T0, psums, g=g):
                for jj, pb in enumerate(psums):
                    cch = g * CGC + gi * 2 + jj
                    yb = getb()
                    P.op("act", lambda e: e.activation(out=yb[:, :], in_=pb[:, :], func=AF.Copy, scale=csc[:, cch:cch + 1]),
                         reads=[pb, csc], writes=[yb])
                    P.dma("sp", actB.t[cch * 128:(cch + 1) * 128, T0:T0 + TT], yb[:, :], yb, reads=[yb], writes=[actB])
            gemm(actA, False, g * CGC * 128, CGC, pair_groups(W["c_w_group"].t[j, g], CGC), ep_sc)

    def mixer_b(layer, j):
        if STAGE in (None, "qkv"):
            with P.scope():
                mixer_b_qkv(layer, j)
        if STAGE in (None, "att"):
            with P.scope():
                mixer_b_att(layer, j)
            out_proj(actB, H, W["b_w_out"].t[j], layer)

    def mixer_b_qkv(layer, j):
        posb = P.sb("posb", [128, TT], I32)
        for tt in range(NT // TT):
            T0 = tt * TT
            P.dma("sp", posb[:, :], pos_in.t[0:1, T0:T0 + TT].broadcast_to([128, TT]), posb, writes=[posb])
            ang = getf()
            P.op("dve", lambda e: e.tensor_copy(out=ang[:, 0:TT], in_=posb[:, :]), reads=[posb], writes=[ang])
            P.op("dve", lambda e: e.tensor_scalar(out=ang[:, 0:TT], in0=ang[:, 0:TT], scalar1=invf[:, 0:1], scalar2=None,
                                                  op0=ALU.mult), reads=[ang, invf], writes=[ang])
            for which in range(2):
                a2 = getf(); nf = getf(); ni = getf()
                P.op("dve", lambda e: e.tensor_scalar(out=a2[:, 0:TT], in0=ang[:, 0:TT], scalar1=1.0,
                                                      scalar2=(math.pi / 2 if which == 0 else 0.0), op0=ALU.mult, op1=ALU.add),
                     reads=[ang], writes=[a2])
                P.op("dve", lambda e: e.tensor_scalar(out=nf[:, 0:TT], in0=a2[:, 0:TT], scalar1=float(1.0 / (2 * math.pi)),
                                                      scalar2=None, op0=ALU.mult), reads=[a2], writes=[nf])
                nii = ni.t[:, 0:TT].bitcast(I32)
                P.op("dve", lambda e: e.tensor_copy(out=nii, in_=nf[:, 0:TT]), reads=[nf], writes=[ni])
                P.op("dve", lambda e: e.tensor_copy(out=nf[:, 0:TT], in_=nii), reads=[ni], writes=[nf])
                P.op("dve", lambda e: e.scalar_tensor_tensor(out=a2[:, 0:TT], in0=nf[:, 0:TT], scalar=-6.28125,
                                                             in1=a2[:, 0:TT], op0=ALU.mult, op1=ALU.add), reads=[nf, a2], writes=[a2])
                P.op("dve", lambda e: e.scalar_tensor_tensor(out=a2[:, 0:TT], in0=nf[:, 0:TT], scalar=float(-(2 * math.pi - 6.28125)),
                                                             in1=a2[:, 0:TT], op0=ALU.mult, op1=ALU.add), reads=[nf, a2], writes=[a2])
                P.op("dve", lambda e: e.tensor_scalar(out=a2[:, 0:TT], in0=a2[:, 0:TT], scalar1=float(math.pi), scalar2=float(-math.pi),
                                                      op0=ALU.min, op1=ALU.max), reads=[a2], writes=[a2])
                P.op("act", lambda e: e.activation(out=nf[:, 0:TT], in_=a2[:, 0:TT], func=AF.Sin), reads=[a2], writes=[nf])
                P.dma("sp", csd.t[which * 128:(which + 1) * 128, T0:T0 + TT], nf[:, 0:TT], nf, reads=[nf], writes=[csd])

        win = W["b_w_in"].t[j]
        def ep_qkv(gi, T0, psums):
            for jj, pb in enumerate(psums):
                n = gi * 2 + jj
                g = n // (3 * H); kind = (n // H) % 3; h = n % H
                row = (g * H + h) * 128
                To = T0 - HAL
                if kind == 2:
                    vb = getb()
                    P.op("act", lambda e: e.activation(out=vb[:, :], in_=pb[:, :], func=AF.Copy), reads=[pb], writes=[vb])
                    P.dma("sp", v_dst.t[row:row + 128, To:To + TT], vb[:, :], vb, reads=[vb], writes=[v_dst])
                else:
                    cb = getf(); sbf = getf()
                    P.dma("sp", cb[:, 0:TT], csd.t[0:128, To:To + TT], cb, reads=[csd], writes=[cb])
                    P.dma("sp", sbf[:, 0:TT], csd.t[128:256, To:To + TT], sbf, reads=[csd], writes=[sbf])
                    qf = getf()
                    P.op("act", lambda e: e.activation(out=qf[:, 0:TT], in_=pb[:, :], func=AF.Copy), reads=[pb], writes=[qf])
                    pr = getpm()
                    P.mm([lambda e: e.matmul(pr[:, :], Rm[:, :], qf[:, 0:TT], start=True, stop=True)], reads=[Rm, qf], writes=[pr])
                    t1 = getf()
                    P.op("dve", lambda e: e.tensor_tensor(out=t1[:, 0:TT], in0=qf[:, 0:TT], in1=cb[:, 0:TT], op=ALU.mult),
                         reads=[qf, cb], writes=[t1])
                    t2 = getf()
                    P.op("dve", lambda e: e.tensor_tensor(out=t2[:, 0:TT], in0=pr[:, :], in1=sbf[:, 0:TT], op=ALU.mult),
                         reads=[pr, sbf], writes=[t2])
                    qb = getb()
                    P.op("dve", lambda e: e.tensor_tensor(out=qb[:, :], in0=t1[:, 0:TT], in1=t2[:, 0:TT], op=ALU.add),
                         reads=[t1, t2], writes=[qb])
                    dstb = q_dst if kind == 0 else k_dst
                    P.dma("sp", dstb.t[row:row + 128, To:To + TT], qb[:, :], qb, reads=[qb], writes=[dstb])

        gemm(xin, True, 0, DC, pair_groups(win, 9 * H), ep_qkv)

    def mixer_b_att(layer, j):
        P.barrier()
        KOFF = NK - NT
        if True:
            acc_o = P.view(arena.t[:, 0:NT], "acc_o", dma=False)
            acc_d = P.view(arena.t[:, NT:2 * NT], "acc_d", dma=False)
            a0 = 2 * NT; hq = NT // 2; hk = NK // 2
            qs = P.view(arena.t[:, a0:a0 + hq].bitcast(BF16), "qs")
            ks = P.view(arena.t[:, a0 + hq:a0 + hq + hk].bitcast(BF16), "ks")
            vs = P.view(arena.t[:, a0 + hq + hk:a0 + hq + 2 * hk].bitcast(BF16), "vs")
            vtk = P.view(arena.t[:, a0 + hq + 2 * hk:a0 + hq + 3 * hk].bitcast(BF16).rearrange("p (b c) -> p b c", c=128),
                         "vtk", dma=False)
            pts = [P.sb("pts%d" % i, [128, 256], BF16, dma=False) for i in range(10)]
            ob = [P.sb("ob%d" % i, [128, TT], BF16) for i in range(2)]
            tilecnt = [0]
            scale = 1.0 / math.sqrt(128.0)
            for h in range(H):
                for g, (window, dil) in enumerate(PATTERNS):
                    row = (g * H + h) * 128
                    L = NK // dil
                    nbl = L // 128
                    fq = (KOFF // dil) // 128
                    P.dma("sp", qs[:, :], q_src.t[row:row + 128, :], qs, reads=[q_src], writes=[qs])
                    P.dma("sp", ks[:, :], k_src.t[row:row + 128, :], ks, reads=[k_src], writes=[ks])
                    P.dma("sp", vs[:, :], v_src.t[row:row + 128, :], vs, reads=[v_src], writes=[vs])
                    qv = qs.t[:, :].rearrange("p (m r) -> p r m", r=dil)
                    kv = ks.t[:, :].rearrange("p (m r) -> p r m", r=dil)
                    vv = vs.t[:, :].rearrange("p (m r) -> p r m", r=dil)
                    aov = acc_o.t[:, :].rearrange("p (m r) -> p r m", r=dil)
                    adv = acc_d.t[:, :].rearrange("p (m r) -> p r m", r=dil)
                    for r in range(dil):
                        for jb0 in range(max(0, fq - 1), nbl, 4):
                            nq = min(4, nbl - jb0)
                            P.mm([lambda e, r=r, jb=jb0 + q, q=q: e.transpose(
                                pt[:, q * 128:(q + 1) * 128], vv[:, r, jb * 128:(jb + 1) * 128], ident[:, :]) for q in range(nq)],
                                reads=[vs, ident], writes=[pt])
                            b0 = r * nbl + jb0
                            P.op("act", lambda e, b0=b0, nq=nq: e.activation(
                                out=vtk[:, b0:b0 + nq, :], in_=pt[:, 0:nq * 128].rearrange("p (q c) -> p q c", q=nq), func=AF.Copy),
                                reads=[pt], writes=[vtk])
                    def phase_a(r, jt0):
                        nq = min(4, nbl - jt0)
                        tilecnt[0] += 1
                        par = tilecnt[0] % 2
                        ptile = {}
                        for kbi, kb in enumerate(range(max(0, jt0 - 1), jt0 + nq)):
                            q0 = max(kb, jt0); q1 = min(kb + 1, jt0 + nq - 1)
                            nqb = q1 - q0 + 1
                            psb = getpm()
                            P.mm([lambda e, kb=kb, q0=q0, nqb=nqb, psb=psb: e.matmul(
                                psb[:, 0:nqb * 128], kv[:, r, kb * 128:(kb + 1) * 128], qv[:, r, (q0 - fq) * 128:(q0 - fq + nqb) * 128],
                                start=True, stop=True)], reads=[ks, qs], writes=[psb])
                            pb_ = pts[par * 5 + kbi]
                            P.op("act", lambda e, psb=psb, pb_=pb_, nqb=nqb: e.activation(
                                out=pb_[:, 0:nqb * 128], in_=psb[:, 0:nqb * 128], func=AF.Exp, scale=scale),
                                reads=[psb], writes=[pb_])
                            m0 = 0 if q0 == kb else 128
                            mk = maskp if kb < fq else maskb
                            P.op("dve", lambda e, pb_=pb_, nqb=nqb, m0=m0, mk=mk: e.tensor_tensor(
                                out=pb_[:, 0:nqb * 128], in0=pb_[:, 0:nqb * 128], in1=mk[:, m0:m0 + nqb * 128], op=ALU.mult),
                                reads=[pb_, mk], writes=[pb_])
                            ptile[kb] = (pb_, q0)
                        return (r, jt0, nq, par, ptile)

                    def phase_b(state):
                        r, jt0, nq, par, ptile = state
                        po = pg[0] if par == 0 else pg[2]
                        pd = pg[1] if par == 0 else pg[3]
                        fo = []; fd = []; rd = [vtk, onesb]
                        for qi in range(nq):
                            qb_ = jt0 + qi
                            srcs = [kb for kb in (qb_ - 1, qb_) if kb >= 0]
                            for si, kb in enumerate(srcs):
                                pb_, q0 = ptile[kb]
                                off = (qb_ - q0) * 128
                                rd.append(pb_)
                                fo.append(lambda e, qi=qi, kb=kb, pb_=pb_, off=off, si=si, ns=len(srcs): e.matmul(
                                    po[:, qi * 128:(qi + 1) * 128], vtk[:, r * nbl + kb, :], pb_[:, off:off + 128],
                                    start=(si == 0), stop=(si == ns - 1)))
                                fd.append(lambda e, qi=qi, pb_=pb_, off=off, si=si, ns=len(srcs): e.matmul(
                                    pd[:, qi * 128:(qi + 1) * 128], onesb[:, :], pb_[:, off:off + 128],
                                    start=(si == 0), stop=(si == ns - 1)))
                        P.mm(fo, reads=rd, writes=[po])
                        P.mm(fd, reads=rd, writes=[pd])
                        ncol = nq * 128
                        m0 = (jt0 - fq) * 128
                        if g == 0:
                            P.op("act", lambda e: e.activation(
                                out=aov[:, r, m0:m0 + ncol], in_=po[:, 0:ncol], func=AF.Copy), reads=[po], writes=[acc_o])
                            P.op("act", lambda e: e.activation(
                                out=adv[:, r, m0:m0 + ncol], in_=pd[:, 0:ncol], func=AF.Copy), reads=[pd], writes=[acc_d])
                        else:
                            P.op("dve", lambda e: e.tensor_tensor(
                                out=aov[:, r, m0:m0 + ncol], in0=aov[:, r, m0:m0 + ncol], in1=po[:, 0:ncol], op=ALU.add),
                                reads=[po, acc_o], writes=[acc_o])
                            P.op("dve", lambda e: e.tensor_tensor(
                                out=adv[:, r, m0:m0 + ncol], in0=adv[:, r, m0:m0 + ncol], in1=pd[:, 0:ncol], op=ALU.add),
                                reads=[pd, acc_d], writes=[acc_d])

                    pending = None
                    for r in range(dil):
                        for jt0 in range(fq, nbl, 4):
                            st_ = phase_a(r, jt0)
                            if pending is not None:
                                phase_b(pending)
                            pending = st_
                    if pending is not None:
                        phase_b(pending)
                for tt in range(NT // TT):
                    T0 = tt * TT
                    rc = getf()
                    P.op("dve", lambda e: e.reciprocal(out=rc[:, 0:TT], in_=acc_d[:, T0:T0 + TT]), reads=[acc_d], writes=[rc])
                    o_ = ob[tt % 2]
                    P.op("dve", lambda e: e.tensor_tensor(out=o_[:, :], in0=acc_o[:, T0:T0 + TT], in1=rc[:, 0:TT], op=ALU.mult),
                         reads=[acc_o, rc], writes=[o_])
                    P.dma("sp", actB.t[h * 128:(h + 1) * 128, HAL + T0:HAL + T0 + TT], o_[:, :], o_, reads=[o_], writes=[actB])
            P.barrier()

    if STAGE is None:
        for cch in range(DC):
            for tt in range(SL // TT):
                T0 = tt * TT
                b = getf()
                P.dma("sp", b[:, 0:TT], xT_in.t[cch * 128:(cch + 1) * 128, T0:T0 + TT], b, writes=[b])
                P.dma("sp", xr.t[cch * 128:(cch + 1) * 128, T0:T0 + TT], b[:, 0:TT], b, reads=[b], writes=[xr])
    stop = cfg.get("STOP")

    def dump_xr():
        for cch in range(DC):
            for tt in range(NT // TT):
                T0 = HAL + tt * TT
                b = getf()
                P.dma("sp", b[:, 0:TT], xr.t[cch * 128:(cch + 1) * 128, T0:T0 + TT], b, reads=[xr], writes=[b])
                P.dma("sp", out_b.t[cch * 128:(cch + 1) * 128, T0 - HAL:T0 - HAL + TT], b[:, 0:TT], b, reads=[b], writes=[out_b])
    for layer in layers:
        kind, j = layer % 3, layer // 3
        if STAGE != "ffn":
            if kind == 0:
                mixer_a(layer, ja.index(j))
            elif kind == 1:
                mixer_b(layer, jb.index(j))
            else:
                mixer_c(layer, jc.index(j))
        if stop == (layer, "mix"):
            dump_xr(); break
        if STAGE in ("mix", "qkv", "att"):
            break
        ffn(layer, layer == layers[-1])
        if stop == (layer, "ffn") and layer != layers[-1]:
            dump_xr(); break
    P.barrier()
    es.close()
    return nc


def _pp(v, nchunk):
    return np.ascontiguousarray(np.asarray(v).reshape(nchunk, 128).T)


def host_constants(cfg, half):
    half_ = 16
    inv_freq = (np.float32(ROPE_THETA) ** (-np.arange(half_, dtype=np.float32) * np.float32(2.0) / np.float32(32))).astype(np.float32)
    cst_f = np.zeros((128, 128 + 128 + 1 + 4 * TT), np.float32)
    for i in range(16):
        cst_f[i + 16, i] = -1.0
        cst_f[i, i + 16] = 1.0
    s = np.arange(128)
    cst_f[:, 128:256] = (s[:, None] <= s[None, :]).astype(np.float32)
    cst_f[0:16, 256] = inv_freq; cst_f[16:32, 256] = inv_freq
    t = np.arange(TT, dtype=np.float32)
    for wi, w in enumerate(C_WINDOWS):
        den = np.minimum(t + 1, np.float32(w)) if half == 0 else np.full(TT, w, np.float32)
        cst_f[:, 257 + wi * TT:257 + (wi + 1) * TT] = (np.float32(1.0) / den)[None, :]
    cst_b = np.zeros((128, 640), np.float32)
    cst_b[:, 0:128] = np.eye(128)
    cst_b[:, 128:256] = (s[:, None] <= s[None, :])
    cst_b[:, 256:384] = (s[:, None] >= s[None, :])
    if half == 1:
        cst_b[:, 384:640] = cst_b[:, 128:384]
    return cst_f, cst_b.astype(ml_dtypes.bfloat16)


def declared_inputs(nc):
    names = []
    for alloc in nc.allocations:
        if isinstance(alloc, mybir.MemoryLocationSet) and alloc.kind == "ExternalInput":
            names.append(alloc.memorylocations[0].name)
    return names


def host_inputs(cfg, inp, layers, xT_list, halves=None, extra=None, pos_list=None):
    c = cfg_derive(cfg)
    D, S, DFF, H, DC, FC, AG = c["D"], c["S"], c["DFF"], c["H"], c["DC"], c["FC"], c["AG"]
    f = lambda k: np.asarray(inp[k], dtype=np.float32)
    assert len(layers) == 1 or layers == list(range(layers[0], layers[-1] + 1))
    l0, l1 = layers[0], layers[-1] + 1
    ja = [l // 3 for l in layers if l % 3 == 0]; jb = [l // 3 for l in layers if l % 3 == 1]; jc = [l // 3 for l in layers if l % 3 == 2]
    sl = lambda js: slice(js[0], js[-1] + 1)
    sh = {}
    if ja:
        sh["a_w_in"] = f("a_w_in")[sl(ja)]; sh["a_w_out"] = f("a_w_out")[sl(ja)]
        sh["a_lnv"] = np.stack([np.concatenate([_pp(f("a_ln_g")[j], DC), _pp(f("a_ln_b")[j], DC)], 1) for j in ja])
        ws = f("a_w_s")[sl(ja)]
        sh["a_wsT"] = np.ascontiguousarray(ws.transpose(0, 3, 1, 2).reshape(len(ja), 128, AG * 128))
        bs = f("a_b_s")[sl(ja)]
        sh["a_bsb"] = np.ascontiguousarray(np.broadcast_to(bs.reshape(len(ja), 1, AG * 128), (len(ja), 128, AG * 128)))
    if jb:
        sh["b_w_in"] = f("b_w_in")[sl(jb)]; sh["b_w_out"] = f("b_w_out")[sl(jb)]
    if jc:
        sh["c_w_in"] = f("c_w_in")[sl(jc)]; sh["c_w_group"] = f("c_w_group")[sl(jc)]; sh["c_w_out"] = f("c_w_out")[sl(jc)]
        sh["c_scale"] = np.stack([_pp(f("c_scale")[j], DC) for j in jc])
    sh["f_w_up"] = f("f_w_up")[l0:l1]; sh["f_w_down"] = f("f_w_down")[l0:l1]
    cw = f("f_conv_w"); cb = f("f_conv_b")
    sh["f_cv"] = np.stack([np.concatenate([_pp(cw[l, 0], 2 * FC), _pp(cw[l, 1], 2 * FC), _pp(cw[l, 2], 2 * FC), _pp(cb[l], 2 * FC)], 1)
                           for l in layers])
    sh["lnv"] = np.stack([np.concatenate([_pp(f("ln_mix_g")[l], DC), _pp(f("ln_mix_b")[l], DC),
                                          _pp(f("ln_ffn_g")[l], DC), _pp(f("ln_ffn_b")[l], DC)], 1) for l in layers])
    consts = [host_constants(cfg, 0), host_constants(cfg, 1)]
    maps = []
    for ci in range(len(xT_list)):
        m = dict(sh)
        hf = 0 if halves is None else halves[ci]
        m["cst_f"], m["cst_b"] = consts[hf]
        m["xT"] = xT_list[ci]
        if pos_list is not None:
            m["pos"] = pos_list[ci]
        if extra is not None:
            m.update(extra[ci])
        maps.append(m)
    return maps


LAUNCH_GROUPS = [[0], [1], [2], [3]]
HALO = 128


def _launch(cfg, inputs, layers, xT_list, **kw):
    nc = build(cfg, layers)
    names = declared_inputs(nc)
    maps = host_inputs(cfg, inputs, layers, xT_list, **kw)
    maps = [{k: m[k] for k in names if k in m} for m in maps]
    return run_bass_kernel_spmd(nc, maps, core_ids=list(range(len(maps)))).results


def kernel_whole_seq(inputs, cfg):
    x = np.asarray(inputs["x"], dtype=np.float32)
    pos = np.asarray(inputs["positions"], dtype=np.int32)
    xT = [np.ascontiguousarray(x[b].T) for b in range(x.shape[0])]
    pl = [np.ascontiguousarray(pos[b][None, :]) for b in range(x.shape[0])]
    for layers in cfg.get("GROUPS", LAUNCH_GROUPS):
        res = _launch(cfg, inputs, layers, xT, pos_list=pl)
        xT = [r["outT"] for r in res]
    return np.stack([np.ascontiguousarray(t.T) for t in xT]).astype(np.float32)


def kernel_split(inputs, cfg):
    x = np.asarray(inputs["x"], dtype=np.float32)
    pos = np.asarray(inputs["positions"], dtype=np.int32)
    B, S, D = x.shape
    NT = S // 2
    halves = [c % 2 for c in range(2 * B)]
    xfull = [np.ascontiguousarray(x[b].T) for b in range(B)]

    def with_halo(full):
        out = []
        for b in range(B):
            for hf in range(2):
                own = full[b][:, hf * NT:(hf + 1) * NT]
                halo = full[b][:, NT - HALO:NT] if hf == 1 else np.zeros((full[b].shape[0], HALO), full[b].dtype)
                out.append(np.ascontiguousarray(np.concatenate([halo, own], axis=1)))
        return out

    def gather(res, key="outT"):
        return [np.ascontiguousarray(np.concatenate([res[2 * b][key], res[2 * b + 1][key]], axis=1)) for b in range(B)]

    base = dict(cfg, HAL=HALO, NT=NT, NK=NT)
    for layer in range(cfg["DEPTH"]):
        kind = layer % 3
        if kind == 1:
            pl = [np.ascontiguousarray(pos[c // 2][None, (c % 2) * NT:(c % 2 + 1) * NT]) for c in range(2 * B)]
            r = _launch(dict(base, STAGE="qkv"), inputs, [layer], with_halo(xfull), halves=halves, pos_list=pl)
            extra = []
            for c in range(2 * B):
                e = {"q_in": r[c]["q_out"]}
                for nm in ("k", "v"):
                    own = r[c][nm + "_out"]
                    prev = r[c - 1][nm + "_out"] if c % 2 == 1 else np.zeros_like(own)
                    e[nm + "_in"] = np.ascontiguousarray(np.concatenate([prev, own], axis=1))
                extra.append(e)
            r = _launch(dict(base, STAGE="att", NK=2 * NT), inputs, [layer], with_halo(xfull), halves=halves, extra=extra)
        else:
            r = _launch(dict(base, STAGE="mix"), inputs, [layer], with_halo(xfull), halves=halves)
        xfull = gather(r)
        r = _launch(dict(base, STAGE="ffn"), inputs, [layer], with_halo(xfull), halves=halves)
        xfull = gather(r)
    return np.stack([np.ascontiguousarray(t.T) for t in xfull]).astype(np.float32)


def kernel(**inputs):
    cfg = CFG
    if cfg.get("MODE", "split") == "split":
        return kernel_split(inputs, cfg)
    return kernel_whole_seq(inputs, cfg)
```
